# Optimizing a Trainium2 kernel written in Bass

```python
import jax, jax.numpy as jnp
from jax import lax
import numpy as np

D_MODEL = 1024
BATCH = 8
SEQ = 2048
DEPTH = 1

GRID_W = 64
N_META = 16
EPS = 1e-6

NA_HEADS = 8
NA_HEAD_DIM = 64
NA_WIDTH = NA_HEADS * NA_HEAD_DIM
NA_WIN_H = 8
NA_WIN_W = 16

HG_HEADS = 4
HG_DK = 128
HG_DV = 128
HG_KDIM = HG_HEADS * HG_DK
HG_VDIM = HG_HEADS * HG_DV
HG_CHUNK = 16

D_FF = 4 * D_MODEL

IN_SPLIT = (NA_WIDTH, NA_WIDTH, NA_WIDTH,
            HG_KDIM, HG_KDIM, HG_KDIM, HG_VDIM, HG_VDIM,
            D_MODEL, D_MODEL)
IN_COLS = sum(IN_SPLIT)

kernel_name = "hybrid_natten_hgrn2_griffin_block"


def rms_norm(x, g):
    xf = x.astype(jnp.float32)
    y = xf * lax.rsqrt(jnp.mean(xf * xf, axis=-1, keepdims=True) + EPS)
    return (y * g.astype(jnp.float32)).astype(x.dtype)


def split_cols(a):
    outs, off = [], 0
    for n in IN_SPLIT:
        outs.append(a[..., off:off + n])
        off += n
    return outs


def neighbourhood_attention(q, k, v, rpb):
    B, L, H, dh = q.shape
    T = L - N_META
    rows = T // GRID_W
    kh = min(NA_WIN_H, rows)
    scale = dh ** -0.5
    qm, km, vm = q[:, :N_META], k[:, :N_META], v[:, :N_META]
    qg = q[:, N_META:].reshape(B, rows, GRID_W, H, dh)
    kg = k[:, N_META:].reshape(B, rows, GRID_W, H, dh)
    vg = v[:, N_META:].reshape(B, rows, GRID_W, H, dh)

    r = jnp.arange(rows)
    row_start = jnp.clip(r - kh // 2, 0, rows - kh)
    row_idx = row_start[:, None] + jnp.arange(kh)[None, :]
    k_blk = kg[:, row_idx]
    v_blk = vg[:, row_idx]

    s_win = jnp.einsum('brchd,brjwhd->bhrcjw', qg, k_blk).astype(jnp.float32) * scale
    c = jnp.arange(GRID_W)
    col_start = jnp.clip(c - NA_WIN_W // 2, 0, GRID_W - NA_WIN_W)
    in_win = (c[None, :] >= col_start[:, None]) & (c[None, :] < col_start[:, None] + NA_WIN_W)
    dr = row_idx - r[:, None]
    dc = jnp.clip(c[None, :] - c[:, None], -(NA_WIN_W - 1), NA_WIN_W - 1)
    bias = rpb.astype(jnp.float32)[:, dr[:, None, :, None] + NA_WIN_H - 1,
                                   dc[None, :, None, :] + NA_WIN_W - 1]
    s_win = jnp.where(in_win[:, None, :], s_win + bias[None], -1e30)

    s_meta = jnp.einsum('brchd,bmhd->bhrcm', qg, km).astype(jnp.float32) * scale
    s = jnp.concatenate([s_win.reshape(B, H, rows, GRID_W, kh * GRID_W), s_meta], axis=-1)
    p = jax.nn.softmax(s, axis=-1).astype(v.dtype)
    p_win = p[..., :kh * GRID_W].reshape(B, H, rows, GRID_W, kh, GRID_W)
    p_meta = p[..., kh * GRID_W:]
    o_grid = (jnp.einsum('bhrcjw,brjwhd->brchd', p_win, v_blk)
              + jnp.einsum('bhrcm,bmhd->brchd', p_meta, vm)).reshape(B, T, H, dh)

    s_mm = jnp.einsum('bmhd,bnhd->bhmn', qm, km).astype(jnp.float32) * scale
    p_mm = jax.nn.softmax(s_mm, axis=-1).astype(v.dtype)
    o_meta = jnp.einsum('bhmn,bnhd->bmhd', p_mm, vm)
    return jnp.concatenate([o_meta, o_grid], axis=1)


def chunk_scan(q, k, v, log_f):
    B, L, H, dk = q.shape
    dv = v.shape[-1]
    n = L // HG_CHUNK

    def to_chunks(a):
        return a.reshape(B, n, HG_CHUNK, H, a.shape[-1]).transpose(1, 0, 3, 2, 4)

    tri = jnp.tril(jnp.ones((HG_CHUNK, HG_CHUNK), dtype=bool))

    def step(S, inp):
        qi, ki, vi, gi = inp
        b = jnp.cumsum(gi, axis=-2)
        o_inter = jnp.einsum('bhtk,bhkv->bhtv', qi * jnp.exp(b), S)
        diff = jnp.where(tri[:, :, None], b[..., :, None, :] - b[..., None, :, :], -jnp.inf)
        A = jnp.einsum('bhtk,bhsk,bhtsk->bhts', qi, ki, jnp.exp(diff))
        o_intra = jnp.einsum('bhts,bhsv->bhtv', A, vi)
        b_last = b[..., -1:, :]
        S_new = (jnp.exp(b_last[..., 0, :])[..., None] * S
                 + jnp.einsum('bhsk,bhsv->bhkv', ki * jnp.exp(b_last - b), vi))
        return S_new, o_inter + o_intra

    S0 = jnp.zeros((B, H, dk, dv), jnp.float32)
    _, o = lax.scan(step, S0, (to_chunks(q), to_chunks(k), to_chunks(v), to_chunks(log_f)))
    return o.transpose(1, 0, 3, 2, 4).reshape(B, L, H, dv)


def hgrn2_branch(q, z_fwd, z_bwd, i, g, lb, gain):
    B, L, _ = q.shape
    dtype = q.dtype

    def heads(a, d):
        return a.astype(jnp.float32).reshape(B, L, HG_HEADS, d)

    qh = jax.nn.silu(heads(q, HG_DK))
    vh = heads(i, HG_DV)

    def gates(z, lb_dir):
        lb_h = lb_dir.reshape(HG_HEADS, HG_DK)
        log_f = jnp.logaddexp(jnp.log(lb_h), jnp.log1p(-lb_h) + jax.nn.log_sigmoid(heads(z, HG_DK)))
        return -jnp.expm1(log_f), log_f

    k_f, lf_f = gates(z_fwd, lb[0])
    k_b, lf_b = gates(z_bwd, lb[1])
    rev = lambda a: jnp.flip(a, axis=1)
    o = chunk_scan(qh, k_f, vh, lf_f) + rev(chunk_scan(rev(qh), rev(k_b), rev(vh), rev(lf_b)))
    o = o * lax.rsqrt(jnp.mean(o * o, axis=-1, keepdims=True) + EPS)
    o = o.reshape(B, L, HG_VDIM) * gain.astype(jnp.float32) * jax.nn.silu(g.astype(jnp.float32))
    return o.astype(dtype)


def setup_inputs(seed: int = 0) -> dict:
    key = jax.random.key(seed)
    ks = jax.random.split(key, 14)
    f32 = jnp.float32
    nrm = lambda k, shape, s: jax.random.normal(k, shape, f32) * s
    return {
        "x": nrm(ks[0], (BATCH, SEQ, D_MODEL), 1.0),
        "meta_tokens": nrm(ks[1], (N_META, D_MODEL), 1.0),
        "w_in": nrm(ks[2], (DEPTH, D_MODEL, IN_COLS), D_MODEL ** -0.5),
        "w_na_out": nrm(ks[3], (DEPTH, NA_WIDTH, D_MODEL), NA_WIDTH ** -0.5),
        "w_hg_out": nrm(ks[4], (DEPTH, HG_VDIM, D_MODEL), HG_VDIM ** -0.5),
        "w_o": nrm(ks[5], (DEPTH, D_MODEL, D_MODEL), D_MODEL ** -0.5),
        "w_up": nrm(ks[6], (DEPTH, D_MODEL, D_FF), D_MODEL ** -0.5),
        "w_down": nrm(ks[7], (DEPTH, D_FF, D_MODEL), D_FF ** -0.5),
        "norm_mix": 1.0 + nrm(ks[8], (DEPTH, D_MODEL), 0.05),
        "norm_mlp": 1.0 + nrm(ks[9], (DEPTH, D_MODEL), 0.05),
        "norm_final": 1.0 + nrm(ks[10], (D_MODEL,), 0.05),
        "hg_norm": 1.0 + nrm(ks[11], (DEPTH, HG_VDIM), 0.05),
        "na_rpb": nrm(ks[12], (DEPTH, NA_HEADS, 2 * NA_WIN_H - 1, 2 * NA_WIN_W - 1), 0.1),
        "hg_lb_logits": nrm(ks[13], (2, DEPTH + 1, HG_KDIM), 0.5),
    }


def reference(x, meta_tokens, w_in, w_na_out, w_hg_out, w_o, w_up, w_down,
              norm_mix, norm_mlp, norm_final, hg_norm, na_rpb, hg_lb_logits):
    B = x.shape[0]
    h = jnp.concatenate([jnp.broadcast_to(meta_tokens.astype(x.dtype)[None], (B, N_META, D_MODEL)), x], axis=1)
    L = h.shape[1]
    lb_all = jnp.cumsum(jax.nn.softmax(hg_lb_logits.astype(jnp.float32), axis=1), axis=1)
    for l in range(DEPTH):
        a = rms_norm(h, norm_mix[l])
        (q_na, k_na, v_na, q_hg, z_f, z_b, i_hg, g_hg, gate_na, gate_hg) = split_cols(a @ w_in[l])
        hd = lambda t: t.reshape(B, L, NA_HEADS, NA_HEAD_DIM)
        y_na = neighbourhood_attention(hd(q_na), hd(k_na), hd(v_na), na_rpb[l]).reshape(B, L, NA_WIDTH) @ w_na_out[l]
        y_hg = hgrn2_branch(q_hg, z_f, z_b, i_hg, g_hg, lb_all[:, l], hg_norm[l]) @ w_hg_out[l]
        mix = jax.nn.sigmoid(gate_na) * y_na + jax.nn.sigmoid(gate_hg) * y_hg
        h = h + mix @ w_o[l]
        m = rms_norm(h, norm_mlp[l])
        h = h + jnp.square(jax.nn.relu(m @ w_up[l])) @ w_down[l]
    h = rms_norm(h, norm_final)
    return h[:, N_META:]
```

```python
import numpy as np
import concourse.bass as bass
import concourse.mybir as mybir
from concourse.bass_utils import run_bass_kernel_spmd

F32 = mybir.dt.float32
BF16 = mybir.dt.bfloat16
U8 = mybir.dt.uint8
AF = mybir.ActivationFunctionType
ALU = mybir.AluOpType
_DSIZE = {F32: 4, BF16: 2, U8: 1}
SBUF_BASE = 16640
SBUF_BYTES = 229376 - SBUF_BASE - 256


class _Op:
    __slots__ = ("idx", "eng", "fn", "deps", "semkey", "val", "inc", "isdma", "waits", "alias")


class Sched:
    def __init__(self, nc, sbuf_bytes):
        self.nc = nc
        self.ops = []
        self.state = {}
        self.inherit = {}
        self.free_list = [(SBUF_BASE, sbuf_bytes)]
        self.freed = []
        self.bufs = {}
        self.hw = 0
        self.uid = 0
        self.dma_sems = []

    def alloc(self, name, shape, dtype):
        self.uid += 1
        name = "%s_%d" % (name, self.uid)
        per_part = int(np.prod(shape[1:])) * _DSIZE[dtype]
        size = (per_part + 63) // 64 * 64
        for i, (off, sz) in enumerate(self.free_list):
            if sz >= size:
                if sz == size:
                    self.free_list.pop(i)
                else:
                    self.free_list[i] = (off + size, sz - size)
                break
        else:
            raise RuntimeError("SBUF arena exhausted allocating %s (%d B)" % (name, size))
        self.hw = max(self.hw, off + size)
        inh = {}
        for (o, s, deps) in self.freed:
            if o < off + size and off < o + s:
                for k, p in deps.items():
                    q = inh.get(k)
                    if q is None or q.idx < p.idx:
                        inh[k] = p
        self.inherit[name] = inh
        self.bufs[name] = (off, size)
        t = self.nc.alloc_sbuf_tensor_at(name, list(shape), dtype, offset=off)
        ap = t.ap() if hasattr(t, "ap") else t[:]
        return name, ap

    def free(self, name):
        off, size = self.bufs.pop(name)
        deps = {}
        for key in list(self.state.keys()):
            base = key[0] if isinstance(key, tuple) else key
            if base == name:
                st = self.state.pop(key)
                cands = list(st[1].values())
                if st[0] is not None:
                    cands.append(st[0])
                for p in cands:
                    q = deps.get(p.semkey)
                    if q is None or q.idx < p.idx:
                        deps[p.semkey] = p
        inh = self.inherit.pop(name, {})
        for k, p in inh.items():
            q = deps.get(k)
            if q is None or q.idx < p.idx:
                deps[k] = p
        self.freed.append((off, size, deps))
        fl = self.free_list + [(off, size)]
        fl.sort()
        merged = []
        for o, s in fl:
            if merged and merged[-1][0] + merged[-1][1] == o:
                merged[-1] = (merged[-1][0], merged[-1][1] + s)
            else:
                merged.append((o, s))
        self.free_list = merged

    def _st(self, key):
        st = self.state.get(key)
        if st is None:
            base = key[0] if isinstance(key, tuple) else key
            st = [None, dict(self.inherit.get(base, {}))]
            self.state[key] = st
        return st

    def add(self, eng, fn, reads=(), writes=(), dma=None):
        op = _Op()
        op.idx = len(self.ops)
        op.eng = eng
        op.fn = fn
        op.isdma = dma is not None
        op.semkey = dma if dma is not None else eng
        op.inc = op.isdma
        op.val = None
        op.waits = None
        op.alias = None
        if op.isdma and dma not in self.dma_sems:
            self.dma_sems.append(dma)
        deps = {}

        def need(p, raw):
            if p is None:
                return
            if p.alias is not None:
                p = p.alias
            if (not op.isdma) and (not p.isdma) and p.eng == eng and not raw:
                return
            p.inc = True
            q = deps.get(p.semkey)
            if q is None or q.idx < p.idx:
                deps[p.semkey] = p

        for r in reads:
            need(self._st(r)[0], True)
        for w in writes:
            st = self._st(w)
            need(st[0], False)
            for p in st[1].values():
                need(p, False)
        for r in reads:
            self._st(r)[1][op.semkey] = op
        for w in writes:
            st = self._st(w)
            st[0] = op
            st[1] = {}
        op.deps = deps
        self.ops.append(op)
        return op

    def set_writer(self, key, op):
        st = self._st(key)
        st[0] = op
        st[1] = {}

    def finalize(self):
        cnt = {}
        waited = {}
        for op in self.ops:
            w = waited.setdefault(op.eng, {})
            waits = []
            for k, p in op.deps.items():
                if w.get(k, 0) < p.val:
                    w[k] = p.val
                    waits.append((k, p.val))
            op.waits = waits
            if op.isdma:
                cnt[op.semkey] = cnt.get(op.semkey, 0) + 16
                op.val = cnt[op.semkey]
            elif op.inc:
                cnt[op.semkey] = cnt.get(op.semkey, 0) + 1
                op.val = cnt[op.semkey]
        self.final_counts = cnt

    def emit(self, eng, handle, sems):
        for op in self.ops:
            if op.eng != eng:
                continue
            for k, v in op.waits:
                handle.wait_ge(sems[k], v)
            ins = op.fn(handle)
            if op.inc:
                ins.then_inc(sems[op.semkey], 16 if op.isdma else 1)


D = 1024
T = 2048
NT = 16
NM = 16
EPS = 1e-6
IN_COLS = 6144
C_QNA, C_KNA, C_VNA = 0, 512, 1024
C_QHG, C_ZF, C_ZB, C_I, C_G = 1536, 2048, 2560, 3072, 3584
C_GNA, C_GHG = 4096, 5120
DFF = 4096


def build_program(debug=False):
    nc = bass.Bass("TRN2", target_bir_lowering=False)
    dt_in = lambda name, shape, dt=F32: nc.dram_tensor(name, list(shape), dt, kind="ExternalInput").ap()
    x = dt_in("x", [T, D])
    meta = dt_in("meta", [NM, D])
    w_in = dt_in("w_in", [D, IN_COLS])
    w_na_out = dt_in("w_na_out", [512, D])
    w_hg_out = dt_in("w_hg_out", [512, D])
    w_o = dt_in("w_o", [D, D])
    w_up = dt_in("w_up", [D, DFF])
    w_down = dt_in("w_down", [DFF, D])
    norm_mix = dt_in("norm_mix", [D])
    norm_mlp = dt_in("norm_mlp", [D])
    norm_final = dt_in("norm_final", [D])
    hg_norm = dt_in("hg_norm", [512])
    lbl_d = dt_in("lbl", [128, 16])
    tab_d = dt_in("tab", [128, 4 * 7 * 2 * 128])
    ident_d = dt_in("ident", [128, 128])
    masks_d = dt_in("masks", [128, 2 * 512], U8)
    out = nc.dram_tensor("out", [T, D], F32, kind="ExternalOutput").ap()
    dbg = {}
    dbg_sems = []

    def tap(name, ap, shape, keys):
        o = nc.dram_tensor("dbg_" + name, list(shape), ap.dtype, kind="ExternalOutput").ap()
        S.add("sp", lambda q: q.dma_start(out=o, in_=ap), reads=keys, dma="d_dbg_" + name)
        dbg_sems.append("d_dbg_" + name)

    S = Sched(nc, SBUF_BYTES)
    ps = [nc.alloc_psum_tensor("ps%d" % i, [128, 512], F32) for i in range(8)]
    PSK = ["ps%d" % i for i in range(8)]

    def psb(i):
        return ps[i][:].bitcast(BF16)

    rec_sink = [None]

    def add(eng, fn, reads=(), writes=(), dma=None):
        if rec_sink[0] is not None:
            assert dma is None
            rec_sink[0].append((eng, fn, list(reads), list(writes)))
            return None
        return S.add(eng, fn, reads=reads, writes=writes, dma=dma)

    def record(f, *a):
        assert rec_sink[0] is None
        rec_sink[0] = []
        f(*a)
        out_ = rec_sink[0]
        rec_sink[0] = None
        return out_

    def merge(main, side):
        if not side:
            return list(main)
        if not main:
            return list(side)
        out_, j = [], 0
        for i, it_ in enumerate(main):
            out_.append(it_)
            tgt = (i + 1) * len(side) // len(main)
            while j < tgt:
                out_.append(side[j]); j += 1
        out_ += side[j:]
        return out_

    def emit_list(lst):
        for it_ in lst:
            add(it_[0], it_[1], reads=it_[2], writes=it_[3])

    cidx = [0]

    def const_dma(dst, src, wkey):
        cidx[0] += 1
        add("sp", lambda q: q.dma_start(out=dst, in_=src), writes=[wkey], dma="d_c%d" % cidx[0])

    n_gb, gb = S.alloc("gb", [128, D], F32)
    n_idb, idb = S.alloc("idb", [128, 128], BF16)
    n_ones, onesb = S.alloc("onesb", [128, 128], BF16)
    n_one1, one1 = S.alloc("one1", [128, 1], F32)
    n_mk, mk = S.alloc("mk", [128, 2, 512], U8)
    n_rs, rs = S.alloc("rs", [128, 96], F32)
    n_lb, lbv = S.alloc("lbv", [128, 16], F32)
    n_lb2, lb2 = S.alloc("lb2", [128, 2, 8], F32)
    n_gainb, gainb = S.alloc("gainb", [128, 512], F32)

    def load_w2(pairs, wkey):
        sem = "d_" + ("_".join(str(x_) for x_ in wkey) if isinstance(wkey, tuple) else str(wkey))
        grp = []
        for j, (dst, src) in enumerate(pairs):
            grp.append(add("pool", lambda q, dst=dst, src=src: q.dma_start(out=dst, in_=src),
                           writes=[wkey] if j == 0 else [], dma=sem))
        for o_ in grp[:-1]:
            o_.alias = grp[-1]
        S.set_writer(wkey, grp[-1])

    def win_cols(c0, n):
        return w_in[:, c0:c0 + n].rearrange("(kc p) c -> p kc c", p=128)

    n_id32, id32 = S.alloc("id32", [128, 128], F32)
    const_dma(gb, norm_mix.partition_broadcast(128), n_gb)
    const_dma(id32, ident_d, n_id32)
    const_dma(mk, masks_d.rearrange("p (a b) -> p a b", a=2), n_mk)
    const_dma(lbv, lbl_d, n_lb)
    const_dma(gainb, hg_norm.partition_broadcast(128), n_gainb)
    add("dve", lambda q: q.tensor_copy(out=idb, in_=id32), reads=[n_id32], writes=[n_idb])
    add("pool", lambda q: q.memset(onesb, 1.0), writes=[n_ones])
    add("pool", lambda q: q.memset(one1, 1.0), writes=[n_one1])
    lv = lbv.rearrange("p (d l h) -> p d l h", d=2, l=2)
    add("dve", lambda q: q.tensor_tensor(out=lb2[:, 0, :].rearrange("p (d h) -> p d h", d=2), in0=lv[:, :, 0, :], in1=lv[:, :, 1, :],
                                         op=ALU.subtract), reads=[n_lb], writes=[n_lb2])
    add("act", lambda q: q.activation(out=lb2[:, 0, :], in_=lb2[:, 0, :], func=AF.Sigmoid), reads=[n_lb2], writes=[n_lb2])
    add("dve", lambda q: q.tensor_scalar(out=lb2[:, 1, :], in0=lb2[:, 0, :], scalar1=-1.0, scalar2=1.0, op0=ALU.mult, op1=ALU.add),
        reads=[n_lb2], writes=[n_lb2])

    def rstd_ops(col0, ncols, P, inv_n):
        sl = rs[:P, col0:col0 + ncols]
        k = [(n_rs, c) for c in range(col0, col0 + ncols)]
        add("dve", lambda q: q.tensor_scalar(out=sl, in0=sl, scalar1=inv_n, scalar2=EPS, op0=ALU.mult, op1=ALU.add), reads=k, writes=k)
        add("act", lambda q: q.activation(out=sl, in_=sl, func=AF.Sqrt), reads=k, writes=k)
        add("dve", lambda q: q.reciprocal(out=sl, in_=sl), reads=k, writes=k)

    n_junk, junk = S.alloc("junk", [128, D], BF16)
    n_xt, xt, n_abf, abf = [], [], [], []
    for i in range(2):
        n, a = S.alloc("abf%d" % i, [128, D], BF16)
        n_abf.append(n); abf.append(a)
    n_aT, aT = S.alloc("aT", [128, 8, T], BF16)
    n_aTm, aTm = S.alloc("aTm", [128, 8, NM], BF16)

    def norm_stats(src_ap, src_keys, P, rcol):
        add("act", lambda q: q.activation(out=junk[:P], in_=src_ap, func=AF.Square, accum_out=rs[:P, rcol:rcol + 1]),
            reads=src_keys, writes=[n_junk, (n_rs, rcol)])
        rstd_ops(rcol, 1, P, 1.0 / D)

    def norm_apply(i, src_ap, src_keys, P, rcol, dst_fn, dst_key, bank, gkey):
        s = i % 2
        add("dve", lambda q: q.scalar_tensor_tensor(out=abf[s][:P], in0=src_ap, scalar=rs[:P, rcol:rcol + 1], in1=gb[:P],
                                                    op0=ALU.mult, op1=ALU.mult),
            reads=list(src_keys) + [(n_rs, rcol), gkey], writes=[n_abf[s]])
        pst = psb(bank)
        for kc in range(8):
            add("pe", lambda q, kc=kc: q.transpose(out=pst[:, kc * P:(kc + 1) * P], in_=abf[s][:P, kc * 128:(kc + 1) * 128],
                                                   identity=idb[:P, :P]),
                reads=[n_abf[s], n_idb], writes=[PSK[bank]])
        add("act", lambda q: q.copy(out=dst_fn(), in_=pst[:, 0:8 * P].rearrange("p (k t) -> p k t", k=8)),
            reads=[PSK[bank]], writes=[PSK[bank], dst_key])

    def x_src(i):
        P = 128 if i < NT else NM
        return P, (x[i * 128:(i + 1) * 128, :] if i < NT else meta)

    n_xall, xall = S.alloc("xall", [128, NT + 1, D], F32)
    add("pool", lambda q: q.memset(rs, 1.0), writes=[(n_rs, c) for c in range(96)])
    for i in range(NT + 1):
        P, src = x_src(i)
        add("sp", lambda q, i=i, P=P, src=src: q.dma_start(out=xall[:P, i, :], in_=src), writes=[(n_xall, i)], dma="d_x%d" % i)

    def x_square(i):
        P, _ = x_src(i)
        add("act", lambda q: q.activation(out=junk[:P], in_=xall[:P, i, :], func=AF.Square, accum_out=rs[:P, i:i + 1]),
            reads=[(n_xall, i)], writes=[n_junk, (n_rs, i)])

    def x_apply(i):
        P, _ = x_src(i)
        if i < NT:
            norm_apply(i, xall[:, i, :], [(n_xall, i)], 128, i, lambda i=i: aT[:, :, i * 128:(i + 1) * 128], (n_aT, i), i % 2, n_gb)
        else:
            norm_apply(i, xall[:NM, i, :], [(n_xall, i)], NM, i, lambda: aTm[:, :, :], n_aTm, i % 2, n_gb)

    for i in range(8):
        x_square(i)
    rstd_ops(0, 8, 128, 1.0 / D)
    for k in range(9):
        x_square(8 + k)
        if k < 8:
            x_apply(k)
    rstd_ops(8, 9, 128, 1.0 / D)
    for i in range(8, NT + 1):
        x_apply(i)
    aT_all = [(n_aT, i) for i in range(NT)]

    def aT_keys(tb):
        return [(n_aT, 4 * tb + j) for j in range(4)]

    if debug:
        tap("aT", aT, [128, 8, T], aT_all)

    S.free(n_xall)
    S.free(n_id32)

    n_yT, yT = S.alloc("yT", [128, 4, T], BF16)
    n_A, A_ = [], []
    for d in range(2):
        n, a = S.alloc("hgA%d" % d, [128, NM + T], F32)
        n_A.append(n); A_.append(a)
    n_Bs, Bs, n_Cs, Cs = [], [], [], []
    for d in range(2):
        n, a = S.alloc("hgB%d" % d, [128, NM + T], F32); n_Bs.append(n); Bs.append(a)
        n, a = S.alloc("hgC%d" % d, [128, NM + T], F32); n_Cs.append(n); Cs.append(a)
    n_qs, qs = S.alloc("qs", [128, T], F32)
    n_tq, tq = [], []
    n, a = S.alloc("tq0", [128, 512], F32)
    n_tq += [n, n]; tq += [a, a]
    n_QTp, QTp, n_KTp, KTp, n_Kt, Kt = [[], []], [[], []], [[], []], [[], []], [], []
    for par in range(2):
        for d in range(2):
            n, a = S.alloc("QT%d%d" % (par, d), [128, T], BF16); n_QTp[par].append(n); QTp[par].append(a)
            n, a = S.alloc("KT%d%d" % (par, d), [128, NM + T], BF16); n_KTp[par].append(n); KTp[par].append(a)
    for d in range(2):
        n, a = S.alloc("Kt%d" % d, [128, NT, 128], BF16); n_Kt.append(n); Kt.append(a)
    n_Ktm, Ktm = S.alloc("Ktm", [NM, 128], BF16)
    n_At, At = S.alloc("At", [128, 2, NT, 128], BF16)
    n_St, St = S.alloc("St", [128, 2, NT, 128], BF16)
    n_Vs, Vhs, n_GGs, GGs, n_Vms, Vms = [], [], [], [], [], []
    n, a = S.alloc("Vh0", [128, NT, 128], BF16); n_Vs += [n, n]; Vhs += [a, a]
    n, a = S.alloc("GG0", [128, NT, 128], BF16); n_GGs += [n, n]; GGs += [a, a]
    n, a = S.alloc("Vm0", [NM, 128], BF16); n_Vms += [n, n]; Vms += [a, a]
    n_tgs, tgs = [], []
    for i in range(2):
        n, a = S.alloc("tg%d" % i, [128, 2, 128], F32); n_tgs.append(n); tgs.append(a)
    n_Z, Z = [], []
    for d in range(2):
        n, a = S.alloc("Z%d" % d, [128, 128], F32); n_Z.append(n); Z.append(a)
    n_gf, gfac = S.alloc("gfac", [128, 2, 16], F32)
    ybf = Kt[0]
    n_WA, WA = S.alloc("WA", [128, 8, 256], BF16)
    n_WB, WB = S.alloc("WB", [128, 8, 128], BF16)
    n_WC, WC = S.alloc("WC", [128, 8, 256], BF16)
    Wz = [WA[:, :, 128:256], WB[:, :, :]]
    Wfm_k = [n_WA, n_WB]
    add("pool", lambda q: q.memset(At.rearrange("p a b c -> p (a b c)"), 0.0), writes=[(n_At, d_, g_) for d_ in range(2) for g_ in range(4)])

    bank_rr = {"proj": [0, 1], "tm": [2, 3]}
    bank_ctr = {"proj": 0, "tm": 0}

    def nb(kind):
        b = bank_rr[kind][bank_ctr[kind] % len(bank_rr[kind])]
        bank_ctr[kind] += 1
        return b

    def hg_weights_fm(hd):
        load_w2([(WA[:, :, jj * 128:(jj + 1) * 128], win_cols(c0 + hd * 128, 128)) for jj, c0 in enumerate((C_QHG, C_ZF))], n_WA)
        load_w2([(WB[:, :, :], win_cols(C_ZB + hd * 128, 128))], n_WB)

    def hg_weights_tm(hd):
        load_w2([(WC[:, :, jj * 128:(jj + 1) * 128], win_cols(c0 + hd * 128, 128)) for jj, c0 in enumerate((C_I, C_G))], n_WC)

    def hg_front(hd):
        Vh, GG, Vm = Vhs[hd % 2], GGs[hd % 2], Vms[hd % 2]
        n_V, n_GG, n_Vm = n_Vs[hd % 2], n_GGs[hd % 2], n_Vms[hd % 2]
        for tb in range(4):
            b = nb("proj")
            for kc in range(8):
                add("pe", lambda q, kc=kc, b=b, tb=tb: q.matmul(ps[b][:, :], lhsT=WA[:, kc, 0:128], rhs=aT[:, kc, tb * 512:(tb + 1) * 512],
                                                               start=(kc == 0), stop=(kc == 7)), reads=aT_keys(tb) + Wfm_k, writes=[PSK[b]])
            t_ = tb % 2
            add("act", lambda q, b=b, t_=t_: q.activation(out=tq[t_], in_=ps[b][:, :], func=AF.Sigmoid), reads=[PSK[b]], writes=[PSK[b], n_tq[t_]])
            add("dve", lambda q, b=b, t_=t_, tb=tb: q.tensor_tensor(out=qs[:, tb * 512:(tb + 1) * 512], in0=ps[b][:, :], in1=tq[t_], op=ALU.mult),
                reads=[PSK[b], n_tq[t_]], writes=[PSK[b], (n_qs, tb)])
        for d in range(2):
            for tb in range(4):
                b = nb("proj")
                for kc in range(8):
                    add("pe", lambda q, kc=kc, b=b, tb=tb, d=d: q.matmul(ps[b][:, :], lhsT=Wz[d][:, kc, :],
                                                                        rhs=aT[:, kc, tb * 512:(tb + 1) * 512], start=(kc == 0), stop=(kc == 7)),
                        reads=aT_keys(tb) + Wfm_k, writes=[PSK[b]])
                add("act", lambda q, b=b, tb=tb, d=d: q.activation(out=A_[d][:, NM + tb * 512:NM + (tb + 1) * 512], in_=ps[b][:, :], func=AF.Sigmoid),
                    reads=[PSK[b]], writes=[PSK[b], (n_A[d], tb)])
            if d == 0:
                b = nb("proj")
                for kc in range(8):
                    add("pe", lambda q, kc=kc, b=b: q.matmul(ps[b][:, 0:NM], lhsT=WA[:, kc, 128:256], rhs=aTm[:, kc, :], start=(kc == 0), stop=(kc == 7)),
                        reads=[n_aTm] + Wfm_k, writes=[PSK[b]])
                add("act", lambda q, b=b: q.activation(out=A_[0][:, 0:NM], in_=ps[b][:, 0:NM], func=AF.Sigmoid), reads=[PSK[b]], writes=[PSK[b], (n_A[0], 4)])

    def hg_front_tm(hd):
        Vh, GG, Vm = Vhs[hd % 2], GGs[hd % 2], Vms[hd % 2]
        n_V, n_GG, n_Vm = n_Vs[hd % 2], n_GGs[hd % 2], n_Vms[hd % 2]
        for n2 in range(NT // 2):
            b = nb("tm")
            for j in range(2):
                n = 2 * n2 + j
                for kc in range(8):
                    add("pe", lambda q, n=n, kc=kc, j=j, b=b: q.matmul(ps[b][:, j * 256:(j + 1) * 256], lhsT=aT[:, kc, n * 128:(n + 1) * 128],
                                                                   rhs=WC[:, kc, :], start=(kc == 0), stop=(kc == 7)),
                        reads=[(n_aT, n), n_WC], writes=[PSK[b]])
            pv_ = ps[b][:].rearrange("p (j c) -> p j c", j=2)
            tg, n_tg = tgs[n2 % 2], n_tgs[n2 % 2]
            add("act", lambda q, pv_=pv_, n2=n2: q.copy(out=Vh[:, 2 * n2:2 * n2 + 2, :], in_=pv_[:, :, 0:128]),
                reads=[PSK[b]], writes=[PSK[b], (n_V, n2)])
            add("act", lambda q, pv_=pv_, tg=tg: q.activation(out=tg, in_=pv_[:, :, 128:256], func=AF.Sigmoid), reads=[PSK[b]], writes=[PSK[b], n_tg])
            add("dve", lambda q, pv_=pv_, tg=tg: q.tensor_tensor(out=tg, in0=pv_[:, :, 128:256], in1=tg, op=ALU.mult), reads=[PSK[b], n_tg], writes=[PSK[b], n_tg])
            add("pool", lambda q, n2=n2, tg=tg: q.tensor_tensor(out=GG[:, 2 * n2:2 * n2 + 2, :], in0=tg,
                                                      in1=gainb[:, hd * 128:(hd + 1) * 128].unsqueeze(1).to_broadcast([128, 2, 128]), op=ALU.mult),
                reads=[n_tg, n_gainb], writes=[(n_GG, n2)])
        b = nb("tm")
        for kc in range(8):
            add("pe", lambda q, kc=kc, b=b: q.matmul(ps[b][:NM, 0:128], lhsT=aTm[:, kc, :], rhs=WC[:, kc, 0:128], start=(kc == 0), stop=(kc == 7)),
                reads=[n_aTm, n_WC], writes=[PSK[b]])
        add("act", lambda q, b=b: q.copy(out=Vm, in_=ps[b][:NM, 0:128]), reads=[PSK[b]], writes=[PSK[b], n_Vm])

    _outer_add = add

    def hg_mid(hd):
        QT, KT, n_QT, n_KT = QTp[hd % 2], KTp[hd % 2], n_QTp[hd % 2], n_KTp[hd % 2]
        rec = [[], []]
        real_add = _outer_add
        for d in range(2):
            B_, C_, n_B, n_C = Bs[d], Cs[d], n_Bs[d], n_Cs[d]

            def add(eng, fn, reads=(), writes=(), _d=d):
                rec[_d].append((eng, fn, list(reads), list(writes)))
            c0 = 0 if d == 0 else NM
            Ak = [(n_A[d], j) for j in range(5 if d == 0 else 4)]
            Ad = A_[d]
            lbc = lb2[:, 0, d * 4 + hd:d * 4 + hd + 1]
            omc = lb2[:, 1, d * 4 + hd:d * 4 + hd + 1]
            add("dve", lambda q, Ad=Ad, c0=c0, lbc=lbc, omc=omc: q.tensor_scalar(out=Ad[:, c0:], in0=Ad[:, c0:], scalar1=omc, scalar2=lbc,
                                                                            op0=ALU.mult, op1=ALU.add), reads=Ak + [n_lb2], writes=Ak)
            add("act", lambda q, Ad=Ad, c0=c0, B_=B_: q.activation(out=B_[:, c0:], in_=Ad[:, c0:], func=AF.Ln), reads=Ak, writes=[n_B])
            add("act", lambda q, Ad=Ad, c0=c0: q.activation(out=Ad[:, c0:], in_=Ad[:, c0:], func=AF.Identity, scale=-1.0, bias=1.0),
                reads=Ak, writes=Ak)
            if d == 0:
                add("dve", lambda q, B_=B_, C_=C_: q.tensor_tensor_scan(out=C_[:, :], data0=one1.to_broadcast([128, NM + T]), data1=B_[:, :], initial=0.0,
                                                          op0=ALU.mult, op1=ALU.add), reads=[n_B, n_one1], writes=[n_C])
                refc = NM + 63
            else:
                add("dve", lambda q, C_=C_: q.memset(C_[:, NM:NM + 1], 0.0), writes=[n_C])
                add("dve", lambda q, B_=B_, C_=C_: q.tensor_tensor_scan(out=C_[:, NM + 1:], data0=one1.to_broadcast([128, T - 1]), data1=B_[:, NM:NM + T - 1],
                                                          initial=0.0, op0=ALU.mult, op1=ALU.add), reads=[n_B, n_one1], writes=[n_C])
                refc = NM + 64
            refs = C_[:, refc:NM + T:128]
            add("pool", lambda q, refs=refs, B_=B_, C_=C_: q.tensor_tensor(out=B_[:, NM:].rearrange("p (n j) -> p n j", n=NT),
                                                            in0=C_[:, NM:].rearrange("p (n j) -> p n j", n=NT),
                                                            in1=refs.unsqueeze(2).to_broadcast([128, NT, 128]), op=ALU.subtract),
                reads=[n_C], writes=[n_B])
            if d == 0:
                add("pool", lambda q, B_=B_, C_=C_: q.tensor_tensor(out=B_[:, 0:NM], in0=C_[:, 0:NM], in1=C_[:, NM - 1:NM].to_broadcast([128, NM]), op=ALU.subtract),
                    reads=[n_C], writes=[n_B])
                add("dve", lambda q, refc=refc, C_=C_: q.tensor_tensor(out=gfac[:, 0, 0:1], in0=C_[:, refc:refc + 1], in1=C_[:, NM - 1:NM], op=ALU.subtract),
                    reads=[n_C], writes=[(n_gf, 0)])
                add("dve", lambda q, refc=refc, C_=C_: q.tensor_tensor(out=gfac[:, 0, 1:16], in0=C_[:, refc + 128:NM + T:128], in1=C_[:, refc:refc + 128 * 15:128],
                                                               op=ALU.subtract), reads=[n_C], writes=[(n_gf, 0)])
            else:
                add("dve", lambda q, refc=refc, C_=C_: q.tensor_tensor(out=gfac[:, 1, 0:15], in0=C_[:, refc + 128:NM + T:128], in1=C_[:, refc:refc + 128 * 15:128],
                                                               op=ALU.subtract), reads=[n_C], writes=[(n_gf, 1)])
                add("dve", lambda q: q.memset(gfac[:, 1, 15:16], 0.0), writes=[(n_gf, 1)])
            add("act", lambda q, d=d: q.activation(out=gfac[:, d, :], in_=gfac[:, d, :], func=AF.Exp), reads=[(n_gf, d)], writes=[(n_gf, d)])
            sq_ = 1.0 if d == 0 else -1.0
            add("act", lambda q, sq_=sq_, B_=B_, C_=C_: q.activation(out=C_[:, NM:], in_=B_[:, NM:], func=AF.Exp, scale=sq_), reads=[n_B], writes=[n_C])
            add("act", lambda q, sq_=sq_, c0=c0, B_=B_: q.activation(out=B_[:, c0:], in_=B_[:, c0:], func=AF.Exp, scale=-sq_), reads=[n_B], writes=[n_B])
            add("pool", lambda q, d=d, C_=C_: q.tensor_tensor(out=QT[d][:, :], in0=qs[:, :], in1=C_[:, NM:], op=ALU.mult),
                reads=[(n_qs, j) for j in range(4)] + [n_C], writes=[n_QT[d]])
            add("dve", lambda q, d=d, c0=c0, Ad=Ad, B_=B_: q.tensor_tensor(out=KT[d][:, c0:], in0=Ad[:, c0:], in1=B_[:, c0:], op=ALU.mult),
                reads=Ak + [n_B], writes=[n_KT[d]])
            rec[d].append(None)
            for half in range(2):
                b = d
                pst = psb(b)
                for j in range(8):
                    n = half * 8 + j
                    add("pe", lambda q, n=n, j=j, d=d, pst=pst: q.transpose(out=pst[:, j * 128:(j + 1) * 128],
                                                                           in_=KT[d][:, NM + n * 128:NM + (n + 1) * 128], identity=idb),
                        reads=[n_KT[d], n_idb], writes=[PSK[b]])
                add("act", lambda q, half=half, d=d, pst=pst: q.copy(out=Kt[d][:, half * 8:(half + 1) * 8, :], in_=pst.rearrange("p (k t) -> p k t", k=8)),
                    reads=[PSK[b]], writes=[PSK[b], (n_Kt[d], half)])
            if d == 0:
                b = d
                pst = psb(b)
                add("pe", lambda q, pst=pst: q.transpose(out=pst[:NM, 0:128], in_=KT[0][:, 0:NM], identity=idb), reads=[n_KT[0], n_idb], writes=[PSK[b]])
                add("act", lambda q, pst=pst: q.copy(out=Ktm, in_=pst[:NM, 0:128]), reads=[PSK[b]], writes=[PSK[b], n_Ktm])
            for g4 in range(4):
                b = 4 + d
                for j in range(4):
                    n = g4 * 4 + j
                    add("pe", lambda q, n=n, j=j, d=d, b=b: q.matmul(ps[b][:, j * 128:(j + 1) * 128], lhsT=KT[d][:, NM + n * 128:NM + (n + 1) * 128],
                                                                    rhs=QT[d][:, n * 128:(n + 1) * 128], start=True, stop=True),
                        reads=[n_KT[d], n_QT[d]], writes=[PSK[b]])
                add("dve", lambda q, g4=g4, d=d, b=b: q.copy_predicated(out=At[:, d, g4 * 4:(g4 + 1) * 4, :].rearrange("p a c -> p (a c)"),
                                                                       mask=mk[:, d, :], data=ps[b][:, :]),
                    reads=[PSK[b], n_mk, (n_At, d, g4)], writes=[PSK[b], (n_At, d, g4)])
        early = [r_[:r_.index(None)] for r_ in rec]
        late = [r_[r_.index(None) + 1:] for r_ in rec]
        return early, late

    def replay(lists):
        import itertools
        for pair in itertools.zip_longest(*lists):
            for it_ in pair:
                if it_ is not None:
                    _outer_add(it_[0], it_[1], reads=it_[2], writes=it_[3])

    def hg_chain(hd):
        Vh, Vm = Vhs[hd % 2], Vms[hd % 2]
        n_V, n_Vm = n_Vs[hd % 2], n_Vms[hd % 2]
        kvb = [[2, 3, 4, 5], [6, 7, 6, 7]]
        b = kvb[0][0]
        add("pe", lambda q, b=b: q.matmul(ps[b][:, 0:128], lhsT=Ktm[:, :], rhs=Vm[:, :], start=True, stop=True),
            reads=[n_Ktm, n_Vm], writes=[PSK[b]])
        add("act", lambda q, b=b: q.copy(out=Z[0], in_=ps[b][:, 0:128]), reads=[PSK[b]], writes=[PSK[b], n_Z[0]])
        orders = [list(range(NT)), list(range(NT - 1, -1, -1))]

        def kv_group(d, g4):
            bd = kvb[d][g4]
            for j in range(4):
                n = orders[d][g4 * 4 + j]
                add("pe", lambda q, n=n, j=j, d=d, b=bd: q.matmul(ps[b][:, j * 128:(j + 1) * 128], lhsT=Kt[d][:, n, :], rhs=Vh[:, n, :],
                                                                 start=True, stop=True),
                    reads=[(n_Kt[d], n // 8), (n_V, n // 2)], writes=[PSK[bd]])

        for g4 in range(2):
            kv_group(1, g4)
        for g4 in range(4):
            kv_group(0, g4)
        for g4 in range(4):
            for j in range(4):
                step = g4 * 4 + j
                if step in (4, 8):
                    kv_group(1, 2 + (step - 4) // 4)
                for d in range(2):
                    n = orders[d][step]
                    b = kvb[d][g4]
                    kvs = ps[b][:, j * 128:(j + 1) * 128]
                    Zd = Z[d]
                    gcol = gfac[:, d, n:n + 1]
                    if d == 0:
                        add("act", lambda q, n=n, gcol=gcol, Zd=Zd: q.activation(out=St[:, 0, n, :], in_=Zd, func=AF.Copy, scale=gcol),
                            reads=[n_Z[0], (n_gf, 0)], writes=[(n_St, 0, n)])
                        if n < NT - 1:
                            add("dve", lambda q, gcol=gcol, kvs=kvs, Zd=Zd: q.scalar_tensor_tensor(out=Zd, in0=Zd, scalar=gcol, in1=kvs, op0=ALU.mult, op1=ALU.add),
                                reads=[n_Z[0], (n_gf, 0), PSK[b]], writes=[n_Z[0], PSK[b]])
                    else:
                        if step == 0:
                            add("act", lambda q, kvs=kvs, Zd=Zd: q.copy(out=Zd, in_=kvs), reads=[PSK[b]], writes=[PSK[b], n_Z[1]])
                        else:
                            add("act", lambda q, n=n, gcol=gcol, Zd=Zd: q.activation(out=St[:, 1, n, :], in_=Zd, func=AF.Copy, scale=gcol),
                                reads=[n_Z[1], (n_gf, 1)], writes=[(n_St, 1, n)])
                            if n > 0:
                                add("dve", lambda q, gcol=gcol, kvs=kvs, Zd=Zd: q.scalar_tensor_tensor(out=Zd, in0=Zd, scalar=gcol, in1=kvs, op0=ALU.mult, op1=ALU.add),
                                    reads=[n_Z[1], (n_gf, 1), PSK[b]], writes=[n_Z[1], PSK[b]])

    def hg_out(hd):
        QT, n_QT = QTp[hd % 2], n_QTp[hd % 2]
        Vh, GG = Vhs[hd % 2], GGs[hd % 2]
        n_V, n_GG = n_Vs[hd % 2], n_GGs[hd % 2]
        ob = [6, 7, 2, 3]
        for g4 in range(4):
            b = ob[g4]
            for j in range(4):
                n = g4 * 4 + j
                o_ = ps[b][:, j * 128:(j + 1) * 128]
                last_b = (n < NT - 1)
                add("pe", lambda q, n=n, o_=o_: q.matmul(o_, lhsT=At[:, 0, n, :], rhs=Vh[:, n, :], start=True, stop=False),
                    reads=[(n_At, 0, n // 4), (n_V, n // 2)], writes=[PSK[b]])
                add("pe", lambda q, n=n, o_=o_: q.matmul(o_, lhsT=At[:, 1, n, :], rhs=Vh[:, n, :], start=False, stop=False),
                    reads=[(n_At, 1, n // 4), (n_V, n // 2)], writes=[PSK[b]])
                add("pe", lambda q, n=n, o_=o_, last_b=last_b: q.matmul(o_, lhsT=QT[0][:, n * 128:(n + 1) * 128], rhs=St[:, 0, n, :], start=False, stop=not last_b),
                    reads=[n_QT[0], (n_St, 0, n)], writes=[PSK[b]])
                if last_b:
                    add("pe", lambda q, n=n, o_=o_: q.matmul(o_, lhsT=QT[1][:, n * 128:(n + 1) * 128], rhs=St[:, 1, n, :], start=False, stop=True),
                        reads=[n_QT[1], (n_St, 1, n)], writes=[PSK[b]])
        for g4 in range(4):
            b = ob[g4]
            for j in range(4):
                rc = 20 + g4 * 4 + j
                add("act", lambda q, j=j, b=b, rc=rc: q.activation(out=junk[:, 0:128], in_=ps[b][:, j * 128:(j + 1) * 128], func=AF.Square,
                                                                  accum_out=rs[:, rc:rc + 1]),
                    reads=[PSK[b]], writes=[PSK[b], n_junk, (n_rs, rc)])
        sl_ = rs[:, 20:36]
        k_ = [(n_rs, c) for c in range(20, 36)]
        add("dve", lambda q: q.tensor_scalar(out=sl_, in0=sl_, scalar1=1.0 / 128, scalar2=EPS, op0=ALU.mult, op1=ALU.add), reads=k_, writes=k_)
        add("act", lambda q: q.activation(out=sl_, in_=sl_, func=AF.Ln), reads=k_, writes=k_)
        add("act", lambda q: q.activation(out=sl_, in_=sl_, func=AF.Exp, scale=-0.5), reads=k_, writes=k_)
        for g4 in range(4):
            b = ob[g4]
            for j in range(4):
                n = g4 * 4 + j
                rc = 20 + n
                add("dve", lambda q, j=j, n=n, b=b, rc=rc: q.scalar_tensor_tensor(out=ybf[:, n, :], in0=ps[b][:, j * 128:(j + 1) * 128],
                                                                                 scalar=rs[:, rc:rc + 1], in1=GG[:, n, :], op0=ALU.mult, op1=ALU.mult),
                    reads=[PSK[b], (n_rs, rc), (n_GG, n // 2)], writes=[PSK[b], (n_Kt[0], n // 8)])
        for half in range(2):
            b2 = nb("proj")
            pst = psb(b2)
            for j in range(8):
                n = half * 8 + j
                add("pe", lambda q, j=j, n=n, pst=pst: q.transpose(out=pst[:, j * 128:(j + 1) * 128], in_=ybf[:, n, :], identity=idb),
                    reads=[(n_Kt[0], n // 8), n_idb], writes=[PSK[b2]])
            add("act", lambda q, half=half, pst=pst: q.copy(out=yT[:, hd, half * 1024:(half + 1) * 1024], in_=pst[:, 0:1024]),
                reads=[PSK[b2]], writes=[PSK[b2], (n_yT, hd, 2 * half), (n_yT, hd, 2 * half + 1)])

    def zipdirs(two):
        import itertools
        return [it_ for pair in itertools.zip_longest(*two) for it_ in pair if it_ is not None]

    hg_weights_fm(0)
    hg_weights_tm(0)
    hg_front(0)
    hg_weights_fm(1)
    early_, late_ = hg_mid(0)
    emit_list(zipdirs(early_))
    for hd in range(4):
        emit_list(merge(zipdirs(late_), record(hg_front_tm, hd)))
        if hd + 1 < 4:
            hg_weights_tm(hd + 1)
        chain_ops = record(hg_chain, hd)
        if hd + 1 < 4:
            front_ops = record(hg_front, hd + 1)
            emit_list(merge(chain_ops, front_ops))
            if hd + 2 < 4:
                hg_weights_fm(hd + 2)
            early_, late_ = hg_mid(hd + 1)
            emit_list(merge(record(hg_out, hd), zipdirs(early_)))
        else:
            emit_list(chain_ops)
            hg_out(hd)
    if debug:
        tap("yT", yT, [128, 4, T], [(n_yT, hd, g4) for hd in range(4) for g4 in range(4)])
    for n in n_A + n_Bs + n_Cs + [n_qs, n_tq[0]] + n_QTp[0] + n_QTp[1] + n_KTp[0] + n_KTp[1] + n_Kt + [n_Ktm, n_At, n_St] + n_tgs + [n_Vs[0], n_GGs[0], n_Vms[0]] + n_Z + [n_gf, n_WA, n_WB, n_WC]:
        S.free(n)

    n_naT, naT = S.alloc("naT", [128, 4, T], BF16)
    n_tab, tab = S.alloc("tab", [128, 4, 7, 2, 128], F32)
    for hp in range(4):
        cidx[0] += 1
        add("sp", lambda q, hp=hp: q.dma_start(out=tab[:, hp].rearrange("p a b c -> p (a b c)"), in_=tab_d[:, hp * 1792:(hp + 1) * 1792]),
            writes=[(n_tab, hp)], dma="d_c%d" % cidx[0])
    NAS = []
    for k_ in range(2):
        d_ = {}
        d_["n_qbd"], d_["qbd"] = S.alloc("qbd%d" % k_, [128, 32, 2, 64], BF16)
        d_["n_kT"], d_["kT"] = S.alloc("kT%d" % k_, [128, NM + T], BF16)
        d_["n_V2"], d_["V2"] = S.alloc("V2%d" % k_, [128, 2, NT, 128], BF16)
        d_["n_Vmn"], d_["Vmn"] = S.alloc("Vmn%d" % k_, [NM, 128], BF16)
        d_["n_W3"], d_["W3"] = S.alloc("W3%d" % k_, [128, 8, 384], BF16)
        NAS.append(d_)
        add("pool", lambda q, qb=d_["qbd"]: q.memset(qb.rearrange("p a b c -> p (a b c)"), 0.0), writes=[(d_["n_qbd"], j) for j in range(4)])
    n_sT, sT, n_pT, pT, n_pTm, pTm, n_rd, rden = [], [], [], [], [], [], [], []
    for i in range(2):
        n, a = S.alloc("sT%d" % i, [128, 512], F32); n_sT.append(n); sT.append(a)
        n, a = S.alloc("pT%d" % i, [128, 512], BF16); n_pT.append(n); pT.append(a)
        n, a = S.alloc("pTm%d" % i, [NM, 128], BF16); n_pTm.append(n); pTm.append(a)
        n, a = S.alloc("rden%d" % i, [128, 128], F32); n_rd.append(n); rden.append(a)
    bank_rr.update({"s": [2, 3], "od": [4, 5], "m": [6, 7]})
    for k_ in ("s", "od", "m"):
        bank_ctr[k_] = 0

    def na_proj_chunks(hp):
        D_ = NAS[hp % 2]
        qbd, kT, V2, Vmn, W3 = D_["qbd"], D_["kT"], D_["V2"], D_["Vmn"], D_["W3"]
        n_qbd, n_kT, n_V2, n_Vmn, n_W3 = D_["n_qbd"], D_["n_kT"], D_["n_V2"], D_["n_Vmn"], D_["n_W3"]
        W3k = [(n_W3, 0), (n_W3, 1)]
        chunks = []

        def c_w():
            blocks = (C_QNA, C_KNA, C_VNA)
            for pi, js in enumerate(((0, 1), (2,))):
                load_w2([(W3[:, :, j * 128:(j + 1) * 128], win_cols(blocks[j] + hp * 128, 128)) for j in js], (n_W3, pi))
        chunks.append(c_w)

        def c_q(tb):
            b = nb("proj")
            for kc in range(8):
                add("pe", lambda q, kc=kc: q.matmul(ps[b][:, :], lhsT=W3[:, kc, 0:128], rhs=aT[:, kc, tb * 512:(tb + 1) * 512],
                                                   start=(kc == 0), stop=(kc == 7)), reads=aT_keys(tb) + W3k, writes=[PSK[b]])
            for h2 in range(2):
                add("act", lambda q, h2=h2: q.activation(out=qbd[h2 * 64:(h2 + 1) * 64, tb * 8:(tb + 1) * 8, h2, :],
                                                         in_=ps[b][h2 * 64:(h2 + 1) * 64, :].rearrange("p (r c) -> p r c", r=8),
                                                         func=AF.Copy, scale=0.125),
                    reads=[PSK[b]], writes=[PSK[b], (n_qbd, tb)])

        def c_k(tb):
            b = nb("proj")
            for kc in range(8):
                add("pe", lambda q, kc=kc: q.matmul(ps[b][:, :], lhsT=W3[:, kc, 128:256], rhs=aT[:, kc, tb * 512:(tb + 1) * 512],
                                                   start=(kc == 0), stop=(kc == 7)), reads=aT_keys(tb) + W3k, writes=[PSK[b]])
            add("act", lambda q: q.copy(out=kT[:, NM + tb * 512:NM + (tb + 1) * 512], in_=ps[b][:, :]),
                reads=[PSK[b]], writes=[PSK[b], (n_kT, tb)])

        def c_km():
            b = nb("proj")
            for kc in range(8):
                add("pe", lambda q, kc=kc: q.matmul(ps[b][:, 0:NM], lhsT=W3[:, kc, 128:256], rhs=aTm[:, kc, :], start=(kc == 0), stop=(kc == 7)),
                    reads=[n_aTm] + W3k, writes=[PSK[b]])
            add("act", lambda q: q.copy(out=kT[:, 0:NM], in_=ps[b][:, 0:NM]), reads=[PSK[b]], writes=[PSK[b], (n_kT, 4)])
            b2 = nb("proj")
            for kc in range(8):
                add("pe", lambda q, kc=kc: q.matmul(ps[b2][:NM, 0:128], lhsT=aTm[:, kc, :], rhs=W3[:, kc, 256:384], start=(kc == 0), stop=(kc == 7)),
                    reads=[n_aTm] + W3k, writes=[PSK[b2]])
            add("act", lambda q: q.copy(out=Vmn, in_=ps[b2][:NM, 0:128]), reads=[PSK[b2]], writes=[PSK[b2], n_Vmn])

        def c_v(par, g4):
            ntile = NT - par
            b = nb("proj")
            js = [j for j in range(g4 * 4, g4 * 4 + 4) if j < ntile]
            for jj, j in enumerate(js):
                st_ = 64 * par + 128 * j
                for kc in range(8):
                    add("pe", lambda q, kc=kc, jj=jj, st_=st_: q.matmul(ps[b][:, jj * 128:(jj + 1) * 128], lhsT=aT[:, kc, st_:st_ + 128],
                                                                       rhs=W3[:, kc, 256:384], start=(kc == 0), stop=(kc == 7)),
                        reads=[(n_aT, st_ // 128), (n_aT, (st_ + 127) // 128)] + W3k, writes=[PSK[b]])
            nn = len(js)
            add("act", lambda q: q.copy(out=V2[:, par, g4 * 4:g4 * 4 + nn, :], in_=ps[b][:, 0:nn * 128].rearrange("p (j c) -> p j c", j=nn)),
                reads=[PSK[b]], writes=[PSK[b], (n_V2, par, g4)])

        for tb in range(4):
            chunks.append(lambda tb=tb: c_q(tb))
        for tb in range(4):
            chunks.append(lambda tb=tb: c_k(tb))
        chunks.append(c_km)
        for par in range(2):
            for g4 in range(4):
                chunks.append(lambda par=par, g4=g4: c_v(par, g4))
        return chunks

    def na_rows(hp, side_chunks):
        D_ = NAS[hp % 2]
        qbd, kT, V2, Vmn = D_["qbd"], D_["kT"], D_["V2"], D_["Vmn"]
        n_qbd, n_kT, n_V2, n_Vmn = D_["n_qbd"], D_["n_kT"], D_["n_V2"], D_["n_Vmn"]
        kT_all = [(n_kT, j) for j in range(5)]
        row_banks = {}

        def qk(r):
            rs_ = min(max(r - 4, 0), 24)
            bs = nb("s"); bm = nb("m")
            row_banks[r] = [bs, bm, None]
            rq = qbd[:, r].rearrange("p a c -> p (a c)")
            for i in range(4):
                k0 = NM + rs_ * 64 + i * 128
                add("pe", lambda q, i=i, k0=k0: q.matmul(ps[bs][:, i * 128:(i + 1) * 128], lhsT=kT[:, k0:k0 + 128], rhs=rq, start=True, stop=True),
                    reads=kT_all + [(n_qbd, r // 8)], writes=[PSK[bs]])
            add("pe", lambda q: q.matmul(ps[bm][:NM, 0:128], lhsT=kT[:, 0:NM], rhs=rq, start=True, stop=True),
                reads=kT_all + [(n_qbd, r // 8)], writes=[PSK[bm]])

        def soft(r):
            rs_ = min(max(r - 4, 0), 24)
            bs, bm, _ = row_banks[r]
            s_ = r % 2
            drb0 = rs_ - r + 7
            j0, par = drb0 // 2, drb0 % 2
            add("dve", lambda q: q.tensor_tensor(out=sT[s_].rearrange("p (i c) -> p i c", i=4),
                                                 in0=ps[bs][:, :].rearrange("p (i c) -> p i c", i=4),
                                                 in1=tab[:, hp, j0:j0 + 4, par, :], op=ALU.add),
                reads=[PSK[bs], (n_tab, hp)], writes=[PSK[bs], n_sT[s_]])
            add("act", lambda q: q.activation(out=pT[s_], in_=sT[s_], func=AF.Exp), reads=[n_sT[s_]], writes=[n_pT[s_]])
            add("act", lambda q: q.activation(out=pTm[s_], in_=ps[bm][:NM, 0:128], func=AF.Exp), reads=[PSK[bm]], writes=[PSK[bm], n_pTm[s_]])

        def pv(r):
            rs_ = min(max(r - 4, 0), 24)
            s_ = r % 2
            par, t0 = rs_ % 2, rs_ // 2
            bo = nb("od")
            row_banks[r][2] = bo
            for i in range(4):
                add("pe", lambda q, i=i: q.matmul(ps[bo][:, 0:128], lhsT=V2[:, par, t0 + i, :], rhs=pT[s_][:, i * 128:(i + 1) * 128],
                                                 start=(i == 0), stop=False),
                    reads=[(n_V2, par, (t0 + i) // 4), n_pT[s_]], writes=[PSK[bo]])
            add("pe", lambda q: q.matmul(ps[bo][:, 0:128], lhsT=Vmn[:, :], rhs=pTm[s_][:, :], start=False, stop=True),
                reads=[n_Vmn, n_pTm[s_]], writes=[PSK[bo]])
            for i in range(4):
                add("pe", lambda q, i=i: q.matmul(ps[bo][:, 128:256], lhsT=onesb[:, :], rhs=pT[s_][:, i * 128:(i + 1) * 128],
                                                 start=(i == 0), stop=False),
                    reads=[n_ones, n_pT[s_]], writes=[PSK[bo]])
            add("pe", lambda q: q.matmul(ps[bo][:, 128:256], lhsT=onesb[:NM, :], rhs=pTm[s_][:, :], start=False, stop=True),
                reads=[n_ones, n_pTm[s_]], writes=[PSK[bo]])

        def evac(r):
            s_ = r % 2
            bo = row_banks[r][2]
            add("dve", lambda q: q.reciprocal(out=rden[s_], in_=ps[bo][:, 128:256]), reads=[PSK[bo]], writes=[PSK[bo], n_rd[s_]])
            for h2 in range(2):
                add("dve", lambda q, h2=h2: q.tensor_tensor(out=naT[h2 * 64:(h2 + 1) * 64, hp, r * 64:(r + 1) * 64],
                                                           in0=ps[bo][h2 * 64:(h2 + 1) * 64, h2 * 64:(h2 + 1) * 64],
                                                           in1=rden[s_][h2 * 64:(h2 + 1) * 64, h2 * 64:(h2 + 1) * 64], op=ALU.mult),
                    reads=[PSK[bo], n_rd[s_]], writes=[PSK[bo], (n_naT, hp, r // 8)])

        side = list(side_chunks)
        qk(0)
        qk(1)
        soft(0)
        for r in range(32):
            if r + 2 < 32:
                qk(r + 2)
            if r + 1 < 32:
                soft(r + 1)
            pv(r)
            evac(r)
            if r % 2 == 1 and side:
                side.pop(0)()
        for c_ in side:
            c_()

    for c_ in na_proj_chunks(0):
        c_()
    for hp in range(4):
        na_rows(hp, na_proj_chunks(hp + 1) if hp + 1 < 4 else [])
    if debug:
        tap("naT", naT, [128, 4, T], [(n_naT, hp, j) for hp in range(4) for j in range(4)])
    for n in [n_tab] + [d_[k_] for d_ in NAS for k_ in ("n_qbd", "n_kT", "n_V2", "n_Vmn", "n_W3")] + n_sT + n_pT + n_pTm + n_rd:
        S.free(n)

    n_Wo, Wo = S.alloc("Wo", [128, 8, D], BF16)

    def load_wo(pc):
        load_w2([(Wo[:, :, pc * 256:(pc + 1) * 256], w_o[:, pc * 256:(pc + 1) * 256].rearrange("(kc p) c -> p kc c", p=128))], (n_Wo, pc))
    n_Wup, Wup, n_Wdn, Wdn, n_tr, tr = [], [], [], [], [], []
    n, a = S.alloc("Wup0", [128, 8, 512], BF16); n_Wup.append(n); Wup.append(a)
    n, a = S.alloc("Wdn0", [128, 4, D], BF16); n_Wdn.append(n); Wdn.append(a)

    def load_up(g, pc):
        s_ = g % 2
        load_w2([(Wup[s_][:, :, pc * 256:(pc + 1) * 256],
                  w_up[:, g * 512 + pc * 256:g * 512 + (pc + 1) * 256].rearrange("(kc p) c -> p kc c", p=128))], (n_Wup[s_], pc))

    def load_dn(g, pc):
        s_ = g % 2
        load_w2([(Wdn[s_][:, 2 * pc:2 * pc + 2, :],
                  w_down[g * 512 + pc * 256:g * 512 + (pc + 1) * 256, :].rearrange("(f p) c -> p f c", p=128))], (n_Wdn[s_], pc))

    def load_group(g):
        for pc in range(2):
            load_up(g, pc)
        for pc in range(2):
            load_dn(g, pc)

    extra_pieces = [lambda pc=pc: load_wo(pc) for pc in range(4)] + [lambda: load_up(0, 0), lambda: load_up(0, 1), lambda: load_dn(0, 0), lambda: load_dn(0, 1)]
    n_mixT, mixT = S.alloc("mixT", [128, 8, T], BF16)
    n_Wg, Wg, n_Wno, Wno, n_Who, Who = [], [], [], [], [], []
    n_sg, sg, n_tt, tt = [], [], [], []
    for i in range(2):
        n, a = S.alloc("Wg%d" % i, [128, 8, 256], BF16); n_Wg.append(n); Wg.append(a)
        n, a = S.alloc("Wno%d" % i, [128, 2, 4, 128], BF16); n_Wno.append(n); Wno.append(a)
    for i in range(4):
        n, a = S.alloc("sg%d" % i, [128, 512], F32); n_sg.append(n); sg.append(a)
        n, a = S.alloc("tt%d" % i, [128, 512], F32); n_tt.append(n); tt.append(a)
    def load_3a(dc):
        s_ = dc % 2
        load_w2([(Wg[s_][:, :, jj * 128:(jj + 1) * 128], win_cols(c0 + dc * 128, 128)) for jj, c0 in enumerate((C_GNA, C_GHG))], n_Wg[s_])
        load_w2([(Wno[s_][:, jj], wsrc[:, dc * 128:(dc + 1) * 128].rearrange("(kc p) c -> p kc c", p=128))
                 for jj, wsrc in enumerate((w_na_out, w_hg_out))], n_Wno[s_])

    it = 0
    for dc in range(8):
        s_ = dc % 2
        if dc == 0:
            load_3a(0)
        if dc + 1 < 8:
            load_3a(dc + 1)
        extra_pieces[dc]()
        for tb in range(4):
            bb = 4 * (it % 2)
            it += 1
            for j in range(2):
                for kc in range(8):
                    add("pe", lambda q, kc=kc, j=j, bb=bb, tb=tb, s_=s_: q.matmul(ps[bb + j][:, :], lhsT=Wg[s_][:, kc, j * 128:(j + 1) * 128],
                                                                                 rhs=aT[:, kc, tb * 512:(tb + 1) * 512], start=(kc == 0), stop=(kc == 7)),
                        reads=aT_keys(tb) + [n_Wg[s_]], writes=[PSK[bb + j]])
            for j, (src, skey) in enumerate(((naT, n_naT), (yT, n_yT))):
                for kc in range(4):
                    add("pe", lambda q, kc=kc, j=j, bb=bb, tb=tb, s_=s_, src=src: q.matmul(ps[bb + 2 + j][:, :], lhsT=Wno[s_][:, j, kc, :],
                                                                                          rhs=src[:, kc, tb * 512:(tb + 1) * 512], start=(kc == 0), stop=(kc == 3)),
                        reads=[(skey, kc, tb), n_Wno[s_]], writes=[PSK[bb + 2 + j]])
            for j in range(2):
                k_ = 2 * (it % 2) + j
                add("act", lambda q, bb=bb, j=j, k_=k_: q.activation(out=sg[k_], in_=ps[bb + j][:, :], func=AF.Sigmoid),
                    reads=[PSK[bb + j]], writes=[PSK[bb + j], n_sg[k_]])
                add("dve", lambda q, bb=bb, j=j, k_=k_: q.tensor_tensor(out=tt[k_], in0=ps[bb + 2 + j][:, :], in1=sg[k_], op=ALU.mult),
                    reads=[PSK[bb + 2 + j], n_sg[k_]], writes=[PSK[bb + 2 + j], n_tt[k_]])
            k0 = 2 * (it % 2)
            add("pool", lambda q, k0=k0, dc=dc, tb=tb: q.tensor_tensor(out=mixT[:, dc, tb * 512:(tb + 1) * 512], in0=tt[k0], in1=tt[k0 + 1], op=ALU.add),
                reads=[n_tt[k0], n_tt[k0 + 1]], writes=[(n_mixT, dc, tb)])
    if debug:
        tap("mixT", mixT, [128, 8, T], [(n_mixT, dc, tb) for dc in range(8) for tb in range(4)])
    for n in [n_aT, n_aTm, n_naT, n_yT] + n_Wg + n_Wno + n_sg + n_tt:
        S.free(n)

    cidx[0] += 1
    add("sp", lambda q: q.dma_start(out=gb, in_=norm_mlp.partition_broadcast(128)), writes=[n_gb], dma="d_c%d" % cidx[0])
    n_h, h = S.alloc("h", [128, NT, D], F32)
    for i in range(NT):
        add("sp", lambda q, i=i: q.dma_start(out=h[:, i, :], in_=x[i * 128:(i + 1) * 128, :]), writes=[(n_h, i, 0), (n_h, i, 1)], dma="d_h%d" % i)
    n_mT, mT = S.alloc("mT", [128, 8, T], BF16)
    bctr = 0
    pending = []
    for i in range(NT):
        for ch in range(2):
            b = 2 + bctr % 6
            bctr += 1
            for dc in range(8):
                add("pe", lambda q, dc=dc, b=b, i=i, ch=ch: q.matmul(ps[b][:, :], lhsT=mixT[:, dc, i * 128:(i + 1) * 128], rhs=Wo[:, dc, ch * 512:(ch + 1) * 512],
                                                                    start=(dc == 0), stop=(dc == 7)),
                    reads=[(n_mixT, dc, i // 4), (n_Wo, 2 * ch), (n_Wo, 2 * ch + 1)], writes=[PSK[b]])
            add("dve", lambda q, b=b, i=i, ch=ch: q.tensor_tensor(out=h[:, i, ch * 512:(ch + 1) * 512], in0=h[:, i, ch * 512:(ch + 1) * 512], in1=ps[b][:, :], op=ALU.add),
                reads=[PSK[b], (n_h, i, ch)], writes=[PSK[b], (n_h, i, ch)])
        add("act", lambda q, i=i: q.activation(out=junk, in_=h[:, i, :], func=AF.Square, accum_out=rs[:, 40 + i:41 + i]),
            reads=[(n_h, i, 0), (n_h, i, 1)], writes=[n_junk, (n_rs, 40 + i)])
        if pending:
            j = pending.pop(0)
            norm_apply(j, h[:, j, :], [(n_h, j, 0), (n_h, j, 1)], 128, 40 + j, lambda j=j: mT[:, :, j * 128:(j + 1) * 128], (n_mT, j), j % 2, n_gb)
        if i % 4 == 3:
            rstd_ops(40 + i - 3, 4, 128, 1.0 / D)
            pending += [i - 3, i - 2, i - 1, i]
    for j in pending:
        norm_apply(j, h[:, j, :], [(n_h, j, 0), (n_h, j, 1)], 128, 40 + j, lambda j=j: mT[:, :, j * 128:(j + 1) * 128], (n_mT, j), j % 2, n_gb)
    if debug:
        tap("h1", h, [128, NT, D], [(n_h, i, ch) for i in range(NT) for ch in range(2)])
    S.free(n_mixT)
    S.free(n_Wo)

    n_uT, uT = S.alloc("uT", [128, 4, T], BF16)
    n, a = S.alloc("Wup1", [128, 8, 512], BF16); n_Wup.append(n); Wup.append(a)
    n, a = S.alloc("Wdn1", [128, 4, D], BF16); n_Wdn.append(n); Wdn.append(a)
    for i in range(3):
        n, a = S.alloc("tr%d" % i, [128, 512], F32); n_tr.append(n); tr.append(a)
    cidx[0] += 1
    add("sp", lambda q: q.dma_start(out=gb, in_=norm_final.partition_broadcast(128)), writes=[n_gb], dma="d_c%d" % cidx[0])
    bank_rr.update({"up": [2, 3, 4], "dn": [5, 6, 7]})
    bank_ctr["up"] = 0
    bank_ctr["dn"] = 0
    trc = 0
    NG = DFF // 512
    for g in range(NG):
        s_ = g % 2
        for fcl in range(4):
            for tb in range(4):
                b = nb("up")
                for kc in range(8):
                    add("pe", lambda q, kc=kc, b=b, fcl=fcl, tb=tb, s_=s_: q.matmul(ps[b][:, :], lhsT=Wup[s_][:, kc, fcl * 128:(fcl + 1) * 128],
                                                                                   rhs=mT[:, kc, tb * 512:(tb + 1) * 512], start=(kc == 0), stop=(kc == 7)),
                        reads=[(n_mT, 4 * tb + j) for j in range(4)] + [(n_Wup[s_], fcl // 2)], writes=[PSK[b]])
                k_ = trc % 3
                trc += 1
                add("act", lambda q, b=b, k_=k_: q.activation(out=tr[k_], in_=ps[b][:, :], func=AF.Relu), reads=[PSK[b]], writes=[PSK[b], n_tr[k_]])
                add("pool", lambda q, k_=k_, fcl=fcl, tb=tb: q.tensor_tensor(out=uT[:, fcl, tb * 512:(tb + 1) * 512], in0=tr[k_], in1=tr[k_], op=ALU.mult),
                    reads=[n_tr[k_]], writes=[(n_uT, fcl, tb)])
        if g + 1 < NG:
            load_group(g + 1)
        for i in range(NT):
            for ch in range(2):
                b = nb("dn")
                for fcl in range(4):
                    add("pe", lambda q, fcl=fcl, b=b, i=i, ch=ch, s_=s_: q.matmul(ps[b][:, :], lhsT=uT[:, fcl, i * 128:(i + 1) * 128],
                                                                                 rhs=Wdn[s_][:, fcl, ch * 512:(ch + 1) * 512], start=(fcl == 0), stop=(fcl == 3)),
                        reads=[(n_uT, fcl, i // 4), (n_Wdn[s_], fcl // 2)], writes=[PSK[b]])
                add("dve", lambda q, b=b, i=i, ch=ch: q.tensor_tensor(out=h[:, i, ch * 512:(ch + 1) * 512], in0=h[:, i, ch * 512:(ch + 1) * 512], in1=ps[b][:, :], op=ALU.add),
                    reads=[PSK[b], (n_h, i, ch)], writes=[PSK[b], (n_h, i, ch)])
            if g == NG - 1:
                rc = 60 + i
                hk = [(n_h, i, 0), (n_h, i, 1)]
                add("act", lambda q, i=i, rc=rc: q.activation(out=junk, in_=h[:, i, :], func=AF.Square, accum_out=rs[:, rc:rc + 1]),
                    reads=hk, writes=[n_junk, (n_rs, rc)])
                rstd_ops(rc, 1, 128, 1.0 / D)
                add("act", lambda q, i=i, rc=rc: q.activation(out=h[:, i, :], in_=h[:, i, :], func=AF.Copy, scale=rs[:, rc:rc + 1]),
                    reads=hk + [(n_rs, rc)], writes=hk)
                add("pool", lambda q, i=i: q.tensor_tensor(out=h[:, i, :], in0=h[:, i, :], in1=gb, op=ALU.mult),
                    reads=hk + [n_gb], writes=hk)
                add("sp", lambda q, i=i: q.dma_start(out=out[i * 128:(i + 1) * 128, :], in_=h[:, i, :]), reads=hk, dma="d_out")
    return nc, S, dbg, ["d_out"] + dbg_sems


def finish_program(nc, S, final_waits):
    from contextlib import ExitStack
    S.finalize()
    with ExitStack() as es:
        sems = {}
        for k in ["pe", "act", "dve", "pool"] + S.dma_sems:
            sems[k] = es.enter_context(nc.semaphore(k))
        block = es.enter_context(nc.Block())

        @block.sync
        def _(q):
            S.emit("sp", q, sems)
            for k in final_waits:
                q.wait_ge(sems[k], S.final_counts[k])

        @block.tensor
        def _(q):
            S.emit("pe", q, sems)

        @block.scalar
        def _(q):
            S.emit("act", q, sems)

        @block.vector
        def _(q):
            S.emit("dve", q, sems)

        @block.gpsimd
        def _(q):
            S.emit("pool", q, sems)
    return nc


def host_inputs(x, meta_tokens, w_in, w_na_out, w_hg_out, w_o, w_up, w_down,
                norm_mix, norm_mlp, norm_final, hg_norm, na_rpb, hg_lb_logits):
    f = lambda a: np.ascontiguousarray(np.asarray(a, dtype=np.float32))
    lbl = f(np.asarray(hg_lb_logits).reshape(2, 2, 4, 128).transpose(3, 0, 1, 2).reshape(128, 16))
    rpb = np.asarray(na_rpb, dtype=np.float32)[0]
    p = np.arange(128)
    jj, w = p // 64, p % 64
    c = np.arange(64)
    cs = np.clip(c - 8, 0, 48)
    in_win = (w[:, None] >= cs[None, :]) & (w[:, None] < cs[None, :] + 16)
    dc = np.clip(w[:, None] - c[None, :], -15, 15) + 15
    tab = np.empty((128, 4, 7, 2, 2, 64), np.float32)
    for hp in range(4):
        for j in range(7):
            for par in range(2):
                drb = 2 * j + par
                dr_idx = drb + jj
                for h2 in range(2):
                    vals = rpb[hp * 2 + h2][dr_idx[:, None], dc]
                    tab[:, hp, j, par, h2, :] = np.where(in_win, vals, np.float32(-1e30))
    tri = np.tril(np.ones((128, 128), np.uint8))
    mf = np.ascontiguousarray(tri.T)
    mb = np.ascontiguousarray(tri)
    masks = np.concatenate([np.tile(mf, (1, 4)), np.tile(mb, (1, 4))], axis=1).astype(np.uint8)
    common = {
        "meta": f(meta_tokens), "w_in": f(np.asarray(w_in)[0]), "w_na_out": f(np.asarray(w_na_out)[0]),
        "w_hg_out": f(np.asarray(w_hg_out)[0]), "w_o": f(np.asarray(w_o)[0]), "w_up": f(np.asarray(w_up)[0]),
        "w_down": f(np.asarray(w_down)[0]), "norm_mix": f(np.asarray(norm_mix)[0]), "norm_mlp": f(np.asarray(norm_mlp)[0]),
        "norm_final": f(norm_final), "hg_norm": f(np.asarray(hg_norm)[0]), "lbl": lbl,
        "tab": np.ascontiguousarray(tab.reshape(128, -1)), "ident": np.eye(128, dtype=np.float32), "masks": masks,
    }
    xs = np.asarray(x, dtype=np.float32)
    return [dict(common, x=np.ascontiguousarray(xs[b])) for b in range(xs.shape[0])]


_CACHE = {}


def kernel(x, meta_tokens, w_in, w_na_out, w_hg_out, w_o, w_up, w_down,
           norm_mix, norm_mlp, norm_final, hg_norm, na_rpb, hg_lb_logits):
    maps = host_inputs(x, meta_tokens, w_in, w_na_out, w_hg_out, w_o, w_up, w_down,
                       norm_mix, norm_mlp, norm_final, hg_norm, na_rpb, hg_lb_logits)
    nc, S, _dbg, fw = build_program(debug=False)
    finish_program(nc, S, fw)
    n = len(maps)
    res = run_bass_kernel_spmd(nc, maps, core_ids=list(range(n)))
    return np.stack([np.asarray(r["out"], dtype=np.float32) for r in res.results], axis=0)
```

```python
import numpy as np
import concourse.bass as bass
import concourse.mybir as mybir
from concourse.bass_utils import run_bass_kernel_spmd

F32 = mybir.dt.float32
BF16 = mybir.dt.bfloat16
U8 = mybir.dt.uint8
AF = mybir.ActivationFunctionType
ALU = mybir.AluOpType
_DSIZE = {F32: 4, BF16: 2, U8: 1}
SBUF_BASE = 16640
SBUF_BYTES = 229376 - SBUF_BASE - 256


class _Op:
    __slots__ = ("idx", "eng", "fn", "deps", "semkey", "val", "inc", "isdma", "waits", "alias")


class Sched:
    def __init__(self, nc, sbuf_bytes):
        self.nc = nc
        self.ops = []
        self.state = {}
        self.inherit = {}
        self.free_list = [(SBUF_BASE, sbuf_bytes)]
        self.freed = []
        self.bufs = {}
        self.hw = 0
        self.uid = 0
        self.dma_sems = []

    def alloc(self, name, shape, dtype):
        self.uid += 1
        name = "%s_%d" % (name, self.uid)
        per_part = int(np.prod(shape[1:])) * _DSIZE[dtype]
        size = (per_part + 63) // 64 * 64
        for i, (off, sz) in enumerate(self.free_list):
            if sz >= size:
                if sz == size:
                    self.free_list.pop(i)
                else:
                    self.free_list[i] = (off + size, sz - size)
                break
        else:
            raise RuntimeError("SBUF arena exhausted allocating %s (%d B)" % (name, size))
        self.hw = max(self.hw, off + size)
        inh = {}
        for (o, s, deps) in self.freed:
            if o < off + size and off < o + s:
                for k, p in deps.items():
                    q = inh.get(k)
                    if q is None or q.idx < p.idx:
                        inh[k] = p
        self.inherit[name] = inh
        self.bufs[name] = (off, size)
        t = self.nc.alloc_sbuf_tensor_at(name, list(shape), dtype, offset=off)
        ap = t.ap() if hasattr(t, "ap") else t[:]
        return name, ap

    def free(self, name):
        off, size = self.bufs.pop(name)
        deps = {}
        for key in list(self.state.keys()):
            base = key[0] if isinstance(key, tuple) else key
            if base == name:
                st = self.state.pop(key)
                cands = list(st[1].values())
                if st[0] is not None:
                    cands.append(st[0])
                for p in cands:
                    q = deps.get(p.semkey)
                    if q is None or q.idx < p.idx:
                        deps[p.semkey] = p
        inh = self.inherit.pop(name, {})
        for k, p in inh.items():
            q = deps.get(k)
            if q is None or q.idx < p.idx:
                deps[k] = p
        self.freed.append((off, size, deps))
        fl = self.free_list + [(off, size)]
        fl.sort()
        merged = []
        for o, s in fl:
            if merged and merged[-1][0] + merged[-1][1] == o:
                merged[-1] = (merged[-1][0], merged[-1][1] + s)
            else:
                merged.append((o, s))
        self.free_list = merged

    def _st(self, key):
        st = self.state.get(key)
        if st is None:
            base = key[0] if isinstance(key, tuple) else key
            st = [None, dict(self.inherit.get(base, {}))]
            self.state[key] = st
        return st

    def add(self, eng, fn, reads=(), writes=(), dma=None):
        op = _Op()
        op.idx = len(self.ops)
        op.eng = eng
        op.fn = fn
        op.isdma = dma is not None
        op.semkey = dma if dma is not None else eng
        op.inc = op.isdma
        op.val = None
        op.waits = None
        op.alias = None
        if op.isdma and dma not in self.dma_sems:
            self.dma_sems.append(dma)
        deps = {}

        def need(p, raw):
            if p is None:
                return
            if p.alias is not None:
                p = p.alias
            if (not op.isdma) and (not p.isdma) and p.eng == eng and not raw:
                return
            p.inc = True
            q = deps.get(p.semkey)
            if q is None or q.idx < p.idx:
                deps[p.semkey] = p

        for r in reads:
            need(self._st(r)[0], True)
        for w in writes:
            st = self._st(w)
            need(st[0], False)
            for p in st[1].values():
                need(p, False)
        for r in reads:
            self._st(r)[1][op.semkey] = op
        for w in writes:
            st = self._st(w)
            st[0] = op
            st[1] = {}
        op.deps = deps
        self.ops.append(op)
        return op

    def set_writer(self, key, op):
        st = self._st(key)
        st[0] = op
        st[1] = {}

    def finalize(self):
        cnt = {}
        waited = {}
        for op in self.ops:
            w = waited.setdefault(op.eng, {})
            waits = []
            for k, p in op.deps.items():
                if w.get(k, 0) < p.val:
                    w[k] = p.val
                    waits.append((k, p.val))
            op.waits = waits
            if op.isdma:
                cnt[op.semkey] = cnt.get(op.semkey, 0) + 16
                op.val = cnt[op.semkey]
            elif op.inc:
                cnt[op.semkey] = cnt.get(op.semkey, 0) + 1
                op.val = cnt[op.semkey]
        self.final_counts = cnt

    def emit(self, eng, handle, sems):
        for op in self.ops:
            if op.eng != eng:
                continue
            for k, v in op.waits:
                handle.wait_ge(sems[k], v)
            ins = op.fn(handle)
            if op.inc:
                ins.then_inc(sems[op.semkey], 16 if op.isdma else 1)


D = 1024
T = 2048
NT = 16
NM = 16
EPS = 1e-6
IN_COLS = 6144
C_QNA, C_KNA, C_VNA = 0, 512, 1024
C_QHG, C_ZF, C_ZB, C_I, C_G = 1536, 2048, 2560, 3072, 3584
C_GNA, C_GHG = 4096, 5120
DFF = 4096


def build_program(debug=False):
    nc = bass.Bass("TRN2", target_bir_lowering=False)
    dt_in = lambda name, shape, dt=F32: nc.dram_tensor(name, list(shape), dt, kind="ExternalInput").ap()
    x = dt_in("x", [T, D])
    meta = dt_in("meta", [NM, D])
    w_in = dt_in("w_in", [D, IN_COLS])
    w_na_out = dt_in("w_na_out", [512, D])
    w_hg_out = dt_in("w_hg_out", [512, D])
    w_o = dt_in("w_o", [D, D])
    w_up = dt_in("w_up", [D, DFF])
    w_down = dt_in("w_down", [DFF, D])
    norm_mix = dt_in("norm_mix", [D])
    norm_mlp = dt_in("norm_mlp", [D])
    norm_final = dt_in("norm_final", [D])
    hg_norm = dt_in("hg_norm", [512])
    lbl_d = dt_in("lbl", [128, 16])
    tab_d = dt_in("tab", [128, 4 * 7 * 2 * 128])
    ident_d = dt_in("ident", [128, 128])
    masks_d = dt_in("masks", [128, 2 * 512], U8)
    out = nc.dram_tensor("out", [T, D], F32, kind="ExternalOutput").ap()
    dbg = {}
    dbg_sems = []

    def tap(name, ap, shape, keys):
        o = nc.dram_tensor("dbg_" + name, list(shape), ap.dtype, kind="ExternalOutput").ap()
        S.add("sp", lambda q: q.dma_start(out=o, in_=ap), reads=keys, dma="d_dbg_" + name)
        dbg_sems.append("d_dbg_" + name)

    S = Sched(nc, SBUF_BYTES)
    ps = [nc.alloc_psum_tensor("ps%d" % i, [128, 512], F32) for i in range(8)]
    PSK = ["ps%d" % i for i in range(8)]

    def psb(i):
        return ps[i][:].bitcast(BF16)

    rec_sink = [None]

    def add(eng, fn, reads=(), writes=(), dma=None):
        if rec_sink[0] is not None:
            assert dma is None
            rec_sink[0].append((eng, fn, list(reads), list(writes)))
            return None
        return S.add(eng, fn, reads=reads, writes=writes, dma=dma)

    def record(f, *a):
        assert rec_sink[0] is None
        rec_sink[0] = []
        f(*a)
        out_ = rec_sink[0]
        rec_sink[0] = None
        return out_

    def merge(main, side):
        if not side:
            return list(main)
        if not main:
            return list(side)
        out_, j = [], 0
        for i, it_ in enumerate(main):
            out_.append(it_)
            tgt = (i + 1) * len(side) // len(main)
            while j < tgt:
                out_.append(side[j]); j += 1
        out_ += side[j:]
        return out_

    def emit_list(lst):
        for it_ in lst:
            add(it_[0], it_[1], reads=it_[2], writes=it_[3])

    cidx = [0]

    def const_dma(dst, src, wkey):
        cidx[0] += 1
        add("sp", lambda q: q.dma_start(out=dst, in_=src), writes=[wkey], dma="d_c%d" % cidx[0])

    n_gb, gb = S.alloc("gb", [128, D], F32)
    n_idb, idb = S.alloc("idb", [128, 128], BF16)
    n_ones, onesb = S.alloc("onesb", [128, 128], BF16)
    n_one1, one1 = S.alloc("one1", [128, 1], F32)
    n_mk, mk = S.alloc("mk", [128, 2, 512], U8)
    n_rs, rs = S.alloc("rs", [128, 96], F32)
    n_lb, lbv = S.alloc("lbv", [128, 16], F32)
    n_lb2, lb2 = S.alloc("lb2", [128, 2, 8], F32)
    n_gainb, gainb = S.alloc("gainb", [128, 512], F32)

    def load_w2(pairs, wkey):
        sem = "d_" + ("_".join(str(x_) for x_ in wkey) if isinstance(wkey, tuple) else str(wkey))
        grp = []
        for j, (dst, src) in enumerate(pairs):
            grp.append(add("pool", lambda q, dst=dst, src=src: q.dma_start(out=dst, in_=src),
                           writes=[wkey] if j == 0 else [], dma=sem))
        for o_ in grp[:-1]:
            o_.alias = grp[-1]
        S.set_writer(wkey, grp[-1])

    def win_cols(c0, n):
        return w_in[:, c0:c0 + n].rearrange("(kc p) c -> p kc c", p=128)

    n_id32, id32 = S.alloc("id32", [128, 128], F32)
    const_dma(gb, norm_mix.partition_broadcast(128), n_gb)
    const_dma(id32, ident_d, n_id32)
    const_dma(mk, masks_d.rearrange("p (a b) -> p a b", a=2), n_mk)
    const_dma(lbv, lbl_d, n_lb)
    const_dma(gainb, hg_norm.partition_broadcast(128), n_gainb)
    add("dve", lambda q: q.tensor_copy(out=idb, in_=id32), reads=[n_id32], writes=[n_idb])
    add("pool", lambda q: q.memset(onesb, 1.0), writes=[n_ones])
    add("pool", lambda q: q.memset(one1, 1.0), writes=[n_one1])
    lv = lbv.rearrange("p (d l h) -> p d l h", d=2, l=2)
    add("dve", lambda q: q.tensor_tensor(out=lb2[:, 0, :].rearrange("p (d h) -> p d h", d=2), in0=lv[:, :, 0, :], in1=lv[:, :, 1, :],
                                         op=ALU.subtract), reads=[n_lb], writes=[n_lb2])
    add("act", lambda q: q.activation(out=lb2[:, 0, :], in_=lb2[:, 0, :], func=AF.Sigmoid), reads=[n_lb2], writes=[n_lb2])
    add("dve", lambda q: q.tensor_scalar(out=lb2[:, 1, :], in0=lb2[:, 0, :], scalar1=-1.0, scalar2=1.0, op0=ALU.mult, op1=ALU.add),
        reads=[n_lb2], writes=[n_lb2])

    def rstd_ops(col0, ncols, P, inv_n):
        sl = rs[:P, col0:col0 + ncols]
        k = [(n_rs, c) for c in range(col0, col0 + ncols)]
        add("dve", lambda q: q.tensor_scalar(out=sl, in0=sl, scalar1=inv_n, scalar2=EPS, op0=ALU.mult, op1=ALU.add), reads=k, writes=k)
        add("act", lambda q: q.activation(out=sl, in_=sl, func=AF.Sqrt), reads=k, writes=k)
        add("dve", lambda q: q.reciprocal(out=sl, in_=sl), reads=k, writes=k)

    n_junk, junk = S.alloc("junk", [128, D], BF16)
    n_xt, xt, n_abf, abf = [], [], [], []
    for i in range(2):
        n, a = S.alloc("abf%d" % i, [128, D], BF16)
        n_abf.append(n); abf.append(a)
    n_aT, aT = S.alloc("aT", [128, 8, T], BF16)
    n_aTm, aTm = S.alloc("aTm", [128, 8, NM], BF16)

    def norm_stats(src_ap, src_keys, P, rcol):
        add("act", lambda q: q.activation(out=junk[:P], in_=src_ap, func=AF.Square, accum_out=rs[:P, rcol:rcol + 1]),
            reads=src_keys, writes=[n_junk, (n_rs, rcol)])
        rstd_ops(rcol, 1, P, 1.0 / D)

    def norm_apply(i, src_ap, src_keys, P, rcol, dst_fn, dst_key, bank, gkey):
        s = i % 2
        add("dve", lambda q: q.scalar_tensor_tensor(out=abf[s][:P], in0=src_ap, scalar=rs[:P, rcol:rcol + 1], in1=gb[:P],
                                                    op0=ALU.mult, op1=ALU.mult),
            reads=list(src_keys) + [(n_rs, rcol), gkey], writes=[n_abf[s]])
        pst = psb(bank)
        for kc in range(8):
            add("pe", lambda q, kc=kc: q.transpose(out=pst[:, kc * P:(kc + 1) * P], in_=abf[s][:P, kc * 128:(kc + 1) * 128],
                                                   identity=idb[:P, :P]),
                reads=[n_abf[s], n_idb], writes=[PSK[bank]])
        add("act", lambda q: q.copy(out=dst_fn(), in_=pst[:, 0:8 * P].rearrange("p (k t) -> p k t", k=8)),
            reads=[PSK[bank]], writes=[PSK[bank], dst_key])

    def x_src(i):
        P = 128 if i < NT else NM
        return P, (x[i * 128:(i + 1) * 128, :] if i < NT else meta)

    n_xall, xall = S.alloc("xall", [128, NT + 1, D], F32)
    add("pool", lambda q: q.memset(rs, 1.0), writes=[(n_rs, c) for c in range(96)])
    for i in range(NT + 1):
        P, src = x_src(i)
        add("sp", lambda q, i=i, P=P, src=src: q.dma_start(out=xall[:P, i, :], in_=src), writes=[(n_xall, i)], dma="d_x%d" % i)

    def x_square(i):
        P, _ = x_src(i)
        add("act", lambda q: q.activation(out=junk[:P], in_=xall[:P, i, :], func=AF.Square, accum_out=rs[:P, i:i + 1]),
            reads=[(n_xall, i)], writes=[n_junk, (n_rs, i)])

    def x_apply(i):
        P, _ = x_src(i)
        if i < NT:
            norm_apply(i, xall[:, i, :], [(n_xall, i)], 128, i, lambda i=i: aT[:, :, i * 128:(i + 1) * 128], (n_aT, i), i % 2, n_gb)
        else:
            norm_apply(i, xall[:NM, i, :], [(n_xall, i)], NM, i, lambda: aTm[:, :, :], n_aTm, i % 2, n_gb)

    for i in range(8):
        x_square(i)
    rstd_ops(0, 8, 128, 1.0 / D)
    for k in range(9):
        x_square(8 + k)
        if k < 8:
            x_apply(k)
    rstd_ops(8, 9, 128, 1.0 / D)
    for i in range(8, NT + 1):
        x_apply(i)
    aT_all = [(n_aT, i) for i in range(NT)]

    def aT_keys(tb):
        return [(n_aT, 4 * tb + j) for j in range(4)]

    if debug:
        tap("aT", aT, [128, 8, T], aT_all)

    S.free(n_xall)
    S.free(n_id32)
    for i in range(2):
        S.free(n_abf[i])

    n_yT, yT = S.alloc("yT", [128, 4, T], BF16)
    n_A, A_ = [], []
    for d in range(2):
        n, a = S.alloc("hgA%d" % d, [128, NM + T], F32)
        n_A.append(n); A_.append(a)
    n_Bs, Bs, n_Cs, Cs = [], [], [], []
    for d in range(2):
        n, a = S.alloc("hgB%d" % d, [128, NM + T], F32); n_Bs.append(n); Bs.append(a)
        n, a = S.alloc("hgC%d" % d, [128, NM + T], F32); n_Cs.append(n); Cs.append(a)
    n_qs, qs = S.alloc("qs", [128, T], F32)
    n_tq, tq = [], []
    n, a = S.alloc("tq0", [128, 512], F32)
    n_tq += [n, n]; tq += [a, a]
    n_QT, QT, n_KT, KT, n_Kt, Kt = [], [], [], [], [], []
    for d in range(2):
        n, a = S.alloc("QT%d" % d, [128, T], BF16); n_QT.append(n); QT.append(a)
        n, a = S.alloc("KT%d" % d, [128, NM + T], BF16); n_KT.append(n); KT.append(a)
        n, a = S.alloc("Kt%d" % d, [128, NT, 128], BF16); n_Kt.append(n); Kt.append(a)
    n_Ktm, Ktm = S.alloc("Ktm", [NM, 128], BF16)
    n_At, At = S.alloc("At", [128, 2, NT, 128], BF16)
    n_St, St = S.alloc("St", [128, 2, NT, 128], BF16)
    n_Vs, Vhs, n_GGs, GGs, n_Vms, Vms = [], [], [], [], [], []
    n, a = S.alloc("Vh0", [128, NT, 128], BF16); n_Vs += [n, n]; Vhs += [a, a]
    n, a = S.alloc("GG0", [128, NT, 128], BF16); n_GGs += [n, n]; GGs += [a, a]
    n, a = S.alloc("Vm0", [NM, 128], BF16); n_Vms += [n, n]; Vms += [a, a]
    n_tgs, tgs = [], []
    for i in range(2):
        n, a = S.alloc("tg%d" % i, [128, 2, 128], F32); n_tgs.append(n); tgs.append(a)
    n_Z, Z = [], []
    for d in range(2):
        n, a = S.alloc("Z%d" % d, [128, 128], F32); n_Z.append(n); Z.append(a)
    n_gf, gfac = S.alloc("gfac", [128, 2, 16], F32)
    ybf = Kt[0]
    n_WA, WA = S.alloc("WA", [128, 8, 256], BF16)
    n_WB, WB = S.alloc("WB", [128, 8, 128], BF16)
    n_WC, WC = S.alloc("WC", [128, 8, 256], BF16)
    Wz = [WA[:, :, 128:256], WB[:, :, :]]
    Wfm_k = [n_WA, n_WB]
    add("pool", lambda q: q.memset(At.rearrange("p a b c -> p (a b c)"), 0.0), writes=[(n_At, d_, g_) for d_ in range(2) for g_ in range(4)])

    bank_rr = {"proj": [0, 1], "tm": [2, 3]}
    bank_ctr = {"proj": 0, "tm": 0}

    def nb(kind):
        b = bank_rr[kind][bank_ctr[kind] % len(bank_rr[kind])]
        bank_ctr[kind] += 1
        return b

    def hg_weights_fm(hd):
        load_w2([(WA[:, :, jj * 128:(jj + 1) * 128], win_cols(c0 + hd * 128, 128)) for jj, c0 in enumerate((C_QHG, C_ZF))], n_WA)
        load_w2([(WB[:, :, :], win_cols(C_ZB + hd * 128, 128))], n_WB)

    def hg_weights_tm(hd):
        load_w2([(WC[:, :, jj * 128:(jj + 1) * 128], win_cols(c0 + hd * 128, 128)) for jj, c0 in enumerate((C_I, C_G))], n_WC)

    def hg_front(hd):
        Vh, GG, Vm = Vhs[hd % 2], GGs[hd % 2], Vms[hd % 2]
        n_V, n_GG, n_Vm = n_Vs[hd % 2], n_GGs[hd % 2], n_Vms[hd % 2]
        for tb in range(4):
            b = nb("proj")
            for kc in range(8):
                add("pe", lambda q, kc=kc, b=b, tb=tb: q.matmul(ps[b][:, :], lhsT=WA[:, kc, 0:128], rhs=aT[:, kc, tb * 512:(tb + 1) * 512],
                                                               start=(kc == 0), stop=(kc == 7)), reads=aT_keys(tb) + Wfm_k, writes=[PSK[b]])
            t_ = tb % 2
            add("act", lambda q, b=b, t_=t_: q.activation(out=tq[t_], in_=ps[b][:, :], func=AF.Sigmoid), reads=[PSK[b]], writes=[PSK[b], n_tq[t_]])
            add("dve", lambda q, b=b, t_=t_, tb=tb: q.tensor_tensor(out=qs[:, tb * 512:(tb + 1) * 512], in0=ps[b][:, :], in1=tq[t_], op=ALU.mult),
                reads=[PSK[b], n_tq[t_]], writes=[PSK[b], (n_qs, tb)])
        for d in range(2):
            for tb in range(4):
                b = nb("proj")
                for kc in range(8):
                    add("pe", lambda q, kc=kc, b=b, tb=tb, d=d: q.matmul(ps[b][:, :], lhsT=Wz[d][:, kc, :],
                                                                        rhs=aT[:, kc, tb * 512:(tb + 1) * 512], start=(kc == 0), stop=(kc == 7)),
                        reads=aT_keys(tb) + Wfm_k, writes=[PSK[b]])
                add("act", lambda q, b=b, tb=tb, d=d: q.activation(out=A_[d][:, NM + tb * 512:NM + (tb + 1) * 512], in_=ps[b][:, :], func=AF.Sigmoid),
                    reads=[PSK[b]], writes=[PSK[b], (n_A[d], tb)])
            if d == 0:
                b = nb("proj")
                for kc in range(8):
                    add("pe", lambda q, kc=kc, b=b: q.matmul(ps[b][:, 0:NM], lhsT=WA[:, kc, 128:256], rhs=aTm[:, kc, :], start=(kc == 0), stop=(kc == 7)),
                        reads=[n_aTm] + Wfm_k, writes=[PSK[b]])
                add("act", lambda q, b=b: q.activation(out=A_[0][:, 0:NM], in_=ps[b][:, 0:NM], func=AF.Sigmoid), reads=[PSK[b]], writes=[PSK[b], (n_A[0], 4)])

    def hg_front_tm(hd):
        Vh, GG, Vm = Vhs[hd % 2], GGs[hd % 2], Vms[hd % 2]
        n_V, n_GG, n_Vm = n_Vs[hd % 2], n_GGs[hd % 2], n_Vms[hd % 2]
        for n2 in range(NT // 2):
            b = nb("tm")
            for j in range(2):
                n = 2 * n2 + j
                for kc in range(8):
                    add("pe", lambda q, n=n, kc=kc, j=j, b=b: q.matmul(ps[b][:, j * 256:(j + 1) * 256], lhsT=aT[:, kc, n * 128:(n + 1) * 128],
                                                                   rhs=WC[:, kc, :], start=(kc == 0), stop=(kc == 7)),
                        reads=[(n_aT, n), n_WC], writes=[PSK[b]])
            pv_ = ps[b][:].rearrange("p (j c) -> p j c", j=2)
            tg, n_tg = tgs[n2 % 2], n_tgs[n2 % 2]
            add("act", lambda q, pv_=pv_, n2=n2: q.copy(out=Vh[:, 2 * n2:2 * n2 + 2, :], in_=pv_[:, :, 0:128]),
                reads=[PSK[b]], writes=[PSK[b], (n_V, n2)])
            add("act", lambda q, pv_=pv_, tg=tg: q.activation(out=tg, in_=pv_[:, :, 128:256], func=AF.Sigmoid), reads=[PSK[b]], writes=[PSK[b], n_tg])
            add("dve", lambda q, pv_=pv_, tg=tg: q.tensor_tensor(out=tg, in0=pv_[:, :, 128:256], in1=tg, op=ALU.mult), reads=[PSK[b], n_tg], writes=[PSK[b], n_tg])
            add("pool", lambda q, n2=n2, tg=tg: q.tensor_tensor(out=GG[:, 2 * n2:2 * n2 + 2, :], in0=tg,
                                                      in1=gainb[:, hd * 128:(hd + 1) * 128].unsqueeze(1).to_broadcast([128, 2, 128]), op=ALU.mult),
                reads=[n_tg, n_gainb], writes=[(n_GG, n2)])
        b = nb("tm")
        for kc in range(8):
            add("pe", lambda q, kc=kc, b=b: q.matmul(ps[b][:NM, 0:128], lhsT=aTm[:, kc, :], rhs=WC[:, kc, 0:128], start=(kc == 0), stop=(kc == 7)),
                reads=[n_aTm, n_WC], writes=[PSK[b]])
        add("act", lambda q, b=b: q.copy(out=Vm, in_=ps[b][:NM, 0:128]), reads=[PSK[b]], writes=[PSK[b], n_Vm])

    _outer_add = add

    def hg_mid(hd):
        rec = [[], []]
        real_add = _outer_add
        for d in range(2):
            B_, C_, n_B, n_C = Bs[d], Cs[d], n_Bs[d], n_Cs[d]

            def add(eng, fn, reads=(), writes=(), _d=d):
                rec[_d].append((eng, fn, list(reads), list(writes)))
            c0 = 0 if d == 0 else NM
            Ak = [(n_A[d], j) for j in range(5 if d == 0 else 4)]
            Ad = A_[d]
            lbc = lb2[:, 0, d * 4 + hd:d * 4 + hd + 1]
            omc = lb2[:, 1, d * 4 + hd:d * 4 + hd + 1]
            add("dve", lambda q, Ad=Ad, c0=c0, lbc=lbc, omc=omc: q.tensor_scalar(out=Ad[:, c0:], in0=Ad[:, c0:], scalar1=omc, scalar2=lbc,
                                                                            op0=ALU.mult, op1=ALU.add), reads=Ak + [n_lb2], writes=Ak)
            add("act", lambda q, Ad=Ad, c0=c0, B_=B_: q.activation(out=B_[:, c0:], in_=Ad[:, c0:], func=AF.Ln), reads=Ak, writes=[n_B])
            add("act", lambda q, Ad=Ad, c0=c0: q.activation(out=Ad[:, c0:], in_=Ad[:, c0:], func=AF.Identity, scale=-1.0, bias=1.0),
                reads=Ak, writes=Ak)
            if d == 0:
                add("dve", lambda q, B_=B_, C_=C_: q.tensor_tensor_scan(out=C_[:, :], data0=one1.to_broadcast([128, NM + T]), data1=B_[:, :], initial=0.0,
                                                          op0=ALU.mult, op1=ALU.add), reads=[n_B, n_one1], writes=[n_C])
                refc = NM + 63
            else:
                add("dve", lambda q, C_=C_: q.memset(C_[:, NM:NM + 1], 0.0), writes=[n_C])
                add("dve", lambda q, B_=B_, C_=C_: q.tensor_tensor_scan(out=C_[:, NM + 1:], data0=one1.to_broadcast([128, T - 1]), data1=B_[:, NM:NM + T - 1],
                                                          initial=0.0, op0=ALU.mult, op1=ALU.add), reads=[n_B, n_one1], writes=[n_C])
                refc = NM + 64
            refs = C_[:, refc:NM + T:128]
            add("pool", lambda q, refs=refs, B_=B_, C_=C_: q.tensor_tensor(out=B_[:, NM:].rearrange("p (n j) -> p n j", n=NT),
                                                            in0=C_[:, NM:].rearrange("p (n j) -> p n j", n=NT),
                                                            in1=refs.unsqueeze(2).to_broadcast([128, NT, 128]), op=ALU.subtract),
                reads=[n_C], writes=[n_B])
            if d == 0:
                add("pool", lambda q, B_=B_, C_=C_: q.tensor_tensor(out=B_[:, 0:NM], in0=C_[:, 0:NM], in1=C_[:, NM - 1:NM].to_broadcast([128, NM]), op=ALU.subtract),
                    reads=[n_C], writes=[n_B])
                add("dve", lambda q, refc=refc, C_=C_: q.tensor_tensor(out=gfac[:, 0, 0:1], in0=C_[:, refc:refc + 1], in1=C_[:, NM - 1:NM], op=ALU.subtract),
                    reads=[n_C], writes=[(n_gf, 0)])
                add("dve", lambda q, refc=refc, C_=C_: q.tensor_tensor(out=gfac[:, 0, 1:16], in0=C_[:, refc + 128:NM + T:128], in1=C_[:, refc:refc + 128 * 15:128],
                                                               op=ALU.subtract), reads=[n_C], writes=[(n_gf, 0)])
            else:
                add("dve", lambda q, refc=refc, C_=C_: q.tensor_tensor(out=gfac[:, 1, 0:15], in0=C_[:, refc + 128:NM + T:128], in1=C_[:, refc:refc + 128 * 15:128],
                                                               op=ALU.subtract), reads=[n_C], writes=[(n_gf, 1)])
                add("dve", lambda q: q.memset(gfac[:, 1, 15:16], 0.0), writes=[(n_gf, 1)])
            add("act", lambda q, d=d: q.activation(out=gfac[:, d, :], in_=gfac[:, d, :], func=AF.Exp), reads=[(n_gf, d)], writes=[(n_gf, d)])
            sq_ = 1.0 if d == 0 else -1.0
            add("act", lambda q, sq_=sq_, B_=B_, C_=C_: q.activation(out=C_[:, NM:], in_=B_[:, NM:], func=AF.Exp, scale=sq_), reads=[n_B], writes=[n_C])
            add("act", lambda q, sq_=sq_, c0=c0, B_=B_: q.activation(out=B_[:, c0:], in_=B_[:, c0:], func=AF.Exp, scale=-sq_), reads=[n_B], writes=[n_B])
            rec[d].append(None)
            add("pool", lambda q, d=d, C_=C_: q.tensor_tensor(out=QT[d][:, :], in0=qs[:, :], in1=C_[:, NM:], op=ALU.mult),
                reads=[(n_qs, j) for j in range(4)] + [n_C], writes=[n_QT[d]])
            add("dve", lambda q, d=d, c0=c0, Ad=Ad, B_=B_: q.tensor_tensor(out=KT[d][:, c0:], in0=Ad[:, c0:], in1=B_[:, c0:], op=ALU.mult),
                reads=Ak + [n_B], writes=[n_KT[d]])
            for half in range(2):
                b = d
                pst = psb(b)
                for j in range(8):
                    n = half * 8 + j
                    add("pe", lambda q, n=n, j=j, d=d, pst=pst: q.transpose(out=pst[:, j * 128:(j + 1) * 128],
                                                                           in_=KT[d][:, NM + n * 128:NM + (n + 1) * 128], identity=idb),
                        reads=[n_KT[d], n_idb], writes=[PSK[b]])
                add("act", lambda q, half=half, d=d, pst=pst: q.copy(out=Kt[d][:, half * 8:(half + 1) * 8, :], in_=pst.rearrange("p (k t) -> p k t", k=8)),
                    reads=[PSK[b]], writes=[PSK[b], (n_Kt[d], half)])
            if d == 0:
                b = d
                pst = psb(b)
                add("pe", lambda q, pst=pst: q.transpose(out=pst[:NM, 0:128], in_=KT[0][:, 0:NM], identity=idb), reads=[n_KT[0], n_idb], writes=[PSK[b]])
                add("act", lambda q, pst=pst: q.copy(out=Ktm, in_=pst[:NM, 0:128]), reads=[PSK[b]], writes=[PSK[b], n_Ktm])
            for g4 in range(4):
                b = 4 + d
                for j in range(4):
                    n = g4 * 4 + j
                    add("pe", lambda q, n=n, j=j, d=d, b=b: q.matmul(ps[b][:, j * 128:(j + 1) * 128], lhsT=KT[d][:, NM + n * 128:NM + (n + 1) * 128],
                                                                    rhs=QT[d][:, n * 128:(n + 1) * 128], start=True, stop=True),
                        reads=[n_KT[d], n_QT[d]], writes=[PSK[b]])
                add("dve", lambda q, g4=g4, d=d, b=b: q.copy_predicated(out=At[:, d, g4 * 4:(g4 + 1) * 4, :].rearrange("p a c -> p (a c)"),
                                                                       mask=mk[:, d, :], data=ps[b][:, :]),
                    reads=[PSK[b], n_mk, (n_At, d, g4)], writes=[PSK[b], (n_At, d, g4)])
        early = [r_[:r_.index(None)] for r_ in rec]
        late = [r_[r_.index(None) + 1:] for r_ in rec]
        return early, late

    def replay(lists):
        import itertools
        for pair in itertools.zip_longest(*lists):
            for it_ in pair:
                if it_ is not None:
                    _outer_add(it_[0], it_[1], reads=it_[2], writes=it_[3])

    def hg_chain(hd):
        Vh, Vm = Vhs[hd % 2], Vms[hd % 2]
        n_V, n_Vm = n_Vs[hd % 2], n_Vms[hd % 2]
        kvb = [[2, 3, 4, 5], [6, 7, 6, 7]]
        b = kvb[0][0]
        add("pe", lambda q, b=b: q.matmul(ps[b][:, 0:128], lhsT=Ktm[:, :], rhs=Vm[:, :], start=True, stop=True),
            reads=[n_Ktm, n_Vm], writes=[PSK[b]])
        add("act", lambda q, b=b: q.copy(out=Z[0], in_=ps[b][:, 0:128]), reads=[PSK[b]], writes=[PSK[b], n_Z[0]])
        orders = [list(range(NT)), list(range(NT - 1, -1, -1))]

        def kv_group(d, g4):
            bd = kvb[d][g4]
            for j in range(4):
                n = orders[d][g4 * 4 + j]
                add("pe", lambda q, n=n, j=j, d=d, b=bd: q.matmul(ps[b][:, j * 128:(j + 1) * 128], lhsT=Kt[d][:, n, :], rhs=Vh[:, n, :],
                                                                 start=True, stop=True),
                    reads=[(n_Kt[d], n // 8), (n_V, n // 2)], writes=[PSK[bd]])

        for g4 in range(2):
            kv_group(1, g4)
        for g4 in range(4):
            kv_group(0, g4)
        for g4 in range(4):
            for j in range(4):
                step = g4 * 4 + j
                if step in (4, 8):
                    kv_group(1, 2 + (step - 4) // 4)
                for d in range(2):
                    n = orders[d][step]
                    b = kvb[d][g4]
                    kvs = ps[b][:, j * 128:(j + 1) * 128]
                    Zd = Z[d]
                    gcol = gfac[:, d, n:n + 1]
                    if d == 0:
                        add("act", lambda q, n=n, gcol=gcol, Zd=Zd: q.activation(out=St[:, 0, n, :], in_=Zd, func=AF.Copy, scale=gcol),
                            reads=[n_Z[0], (n_gf, 0)], writes=[(n_St, 0, n)])
                        if n < NT - 1:
                            add("dve", lambda q, gcol=gcol, kvs=kvs, Zd=Zd: q.scalar_tensor_tensor(out=Zd, in0=Zd, scalar=gcol, in1=kvs, op0=ALU.mult, op1=ALU.add),
                                reads=[n_Z[0], (n_gf, 0), PSK[b]], writes=[n_Z[0], PSK[b]])
                    else:
                        if step == 0:
                            add("act", lambda q, kvs=kvs, Zd=Zd: q.copy(out=Zd, in_=kvs), reads=[PSK[b]], writes=[PSK[b], n_Z[1]])
                        else:
                            add("act", lambda q, n=n, gcol=gcol, Zd=Zd: q.activation(out=St[:, 1, n, :], in_=Zd, func=AF.Copy, scale=gcol),
                                reads=[n_Z[1], (n_gf, 1)], writes=[(n_St, 1, n)])
                            if n > 0:
                                add("dve", lambda q, gcol=gcol, kvs=kvs, Zd=Zd: q.scalar_tensor_tensor(out=Zd, in0=Zd, scalar=gcol, in1=kvs, op0=ALU.mult, op1=ALU.add),
                                    reads=[n_Z[1], (n_gf, 1), PSK[b]], writes=[n_Z[1], PSK[b]])

    def hg_out(hd):
        Vh, GG = Vhs[hd % 2], GGs[hd % 2]
        n_V, n_GG = n_Vs[hd % 2], n_GGs[hd % 2]
        ob = [6, 7, 2, 3]
        for g4 in range(4):
            b = ob[g4]
            for j in range(4):
                n = g4 * 4 + j
                o_ = ps[b][:, j * 128:(j + 1) * 128]
                last_b = (n < NT - 1)
                add("pe", lambda q, n=n, o_=o_: q.matmul(o_, lhsT=At[:, 0, n, :], rhs=Vh[:, n, :], start=True, stop=False),
                    reads=[(n_At, 0, n // 4), (n_V, n // 2)], writes=[PSK[b]])
                add("pe", lambda q, n=n, o_=o_: q.matmul(o_, lhsT=At[:, 1, n, :], rhs=Vh[:, n, :], start=False, stop=False),
                    reads=[(n_At, 1, n // 4), (n_V, n // 2)], writes=[PSK[b]])
                add("pe", lambda q, n=n, o_=o_, last_b=last_b: q.matmul(o_, lhsT=QT[0][:, n * 128:(n + 1) * 128], rhs=St[:, 0, n, :], start=False, stop=not last_b),
                    reads=[n_QT[0], (n_St, 0, n)], writes=[PSK[b]])
                if last_b:
                    add("pe", lambda q, n=n, o_=o_: q.matmul(o_, lhsT=QT[1][:, n * 128:(n + 1) * 128], rhs=St[:, 1, n, :], start=False, stop=True),
                        reads=[n_QT[1], (n_St, 1, n)], writes=[PSK[b]])
        for g4 in range(4):
            b = ob[g4]
            for j in range(4):
                rc = 20 + g4 * 4 + j
                add("act", lambda q, j=j, b=b, rc=rc: q.activation(out=junk[:, 0:128], in_=ps[b][:, j * 128:(j + 1) * 128], func=AF.Square,
                                                                  accum_out=rs[:, rc:rc + 1]),
                    reads=[PSK[b]], writes=[PSK[b], n_junk, (n_rs, rc)])
        sl_ = rs[:, 20:36]
        k_ = [(n_rs, c) for c in range(20, 36)]
        add("dve", lambda q: q.tensor_scalar(out=sl_, in0=sl_, scalar1=1.0 / 128, scalar2=EPS, op0=ALU.mult, op1=ALU.add), reads=k_, writes=k_)
        add("act", lambda q: q.activation(out=sl_, in_=sl_, func=AF.Ln), reads=k_, writes=k_)
        add("act", lambda q: q.activation(out=sl_, in_=sl_, func=AF.Exp, scale=-0.5), reads=k_, writes=k_)
        for g4 in range(4):
            b = ob[g4]
            for j in range(4):
                n = g4 * 4 + j
                rc = 20 + n
                add("dve", lambda q, j=j, n=n, b=b, rc=rc: q.scalar_tensor_tensor(out=ybf[:, n, :], in0=ps[b][:, j * 128:(j + 1) * 128],
                                                                                 scalar=rs[:, rc:rc + 1], in1=GG[:, n, :], op0=ALU.mult, op1=ALU.mult),
                    reads=[PSK[b], (n_rs, rc), (n_GG, n // 2)], writes=[PSK[b], (n_Kt[0], n // 8)])
        for half in range(2):
            b2 = nb("proj")
            pst = psb(b2)
            for j in range(8):
                n = half * 8 + j
                add("pe", lambda q, j=j, n=n, pst=pst: q.transpose(out=pst[:, j * 128:(j + 1) * 128], in_=ybf[:, n, :], identity=idb),
                    reads=[(n_Kt[0], n // 8), n_idb], writes=[PSK[b2]])
            add("act", lambda q, half=half, pst=pst: q.copy(out=yT[:, hd, half * 1024:(half + 1) * 1024], in_=pst[:, 0:1024]),
                reads=[PSK[b2]], writes=[PSK[b2], (n_yT, hd, 2 * half), (n_yT, hd, 2 * half + 1)])

    NAS = []

    def alloc_nas(k_):
        d_ = {}
        d_["n_qbd"], d_["qbd"] = S.alloc("qbd%d" % k_, [128, 32, 2, 64], BF16)
        d_["n_kT"], d_["kT"] = S.alloc("kT%d" % k_, [128, NM + T], BF16)
        d_["n_V2"], d_["V2"] = S.alloc("V2%d" % k_, [128, 2, NT, 128], BF16)
        d_["n_Vmn"], d_["Vmn"] = S.alloc("Vmn%d" % k_, [NM, 128], BF16)
        d_["n_W3"], d_["W3"] = S.alloc("W3%d" % k_, [128, 8, 384], BF16)
        NAS.append(d_)
        add("pool", lambda q, qb=d_["qbd"]: q.memset(qb.rearrange("p a b c -> p (a b c)"), 0.0), writes=[(d_["n_qbd"], j) for j in range(4)])

    alloc_nas(0)
    bank_rr["nap"] = [4, 5]
    bank_ctr["nap"] = 0

    def na_proj_chunks(hp, kinds=("proj", "proj")):
        D_ = NAS[hp % 2]
        qbd, kT, V2, Vmn, W3 = D_["qbd"], D_["kT"], D_["V2"], D_["Vmn"], D_["W3"]
        n_qbd, n_kT, n_V2, n_Vmn, n_W3 = D_["n_qbd"], D_["n_kT"], D_["n_V2"], D_["n_Vmn"], D_["n_W3"]
        W3k = [(n_W3, 0), (n_W3, 1)]
        chunks = []

        def c_w():
            blocks = (C_QNA, C_KNA, C_VNA)
            for pi, js in enumerate(((0, 1), (2,))):
                load_w2([(W3[:, :, j * 128:(j + 1) * 128], win_cols(blocks[j] + hp * 128, 128)) for j in js], (n_W3, pi))
        chunks.append(c_w)

        def c_q(tb):
            b = nb(kinds[0])
            for kc in range(8):
                add("pe", lambda q, kc=kc: q.matmul(ps[b][:, :], lhsT=W3[:, kc, 0:128], rhs=aT[:, kc, tb * 512:(tb + 1) * 512],
                                                   start=(kc == 0), stop=(kc == 7)), reads=aT_keys(tb) + W3k, writes=[PSK[b]])
            for h2 in range(2):
                add("act", lambda q, h2=h2: q.activation(out=qbd[h2 * 64:(h2 + 1) * 64, tb * 8:(tb + 1) * 8, h2, :],
                                                         in_=ps[b][h2 * 64:(h2 + 1) * 64, :].rearrange("p (r c) -> p r c", r=8),
                                                         func=AF.Copy, scale=0.125),
                    reads=[PSK[b]], writes=[PSK[b], (n_qbd, tb)])

        def c_k(tb):
            b = nb(kinds[0])
            for kc in range(8):
                add("pe", lambda q, kc=kc: q.matmul(ps[b][:, :], lhsT=W3[:, kc, 128:256], rhs=aT[:, kc, tb * 512:(tb + 1) * 512],
                                                   start=(kc == 0), stop=(kc == 7)), reads=aT_keys(tb) + W3k, writes=[PSK[b]])
            add("act", lambda q: q.copy(out=kT[:, NM + tb * 512:NM + (tb + 1) * 512], in_=ps[b][:, :]),
                reads=[PSK[b]], writes=[PSK[b], (n_kT, tb)])

        def c_km():
            b = nb(kinds[0])
            for kc in range(8):
                add("pe", lambda q, kc=kc: q.matmul(ps[b][:, 0:NM], lhsT=W3[:, kc, 128:256], rhs=aTm[:, kc, :], start=(kc == 0), stop=(kc == 7)),
                    reads=[n_aTm] + W3k, writes=[PSK[b]])
            add("act", lambda q: q.copy(out=kT[:, 0:NM], in_=ps[b][:, 0:NM]), reads=[PSK[b]], writes=[PSK[b], (n_kT, 4)])
            b2 = nb(kinds[0])
            for kc in range(8):
                add("pe", lambda q, kc=kc: q.matmul(ps[b2][:NM, 0:128], lhsT=aTm[:, kc, :], rhs=W3[:, kc, 256:384], start=(kc == 0), stop=(kc == 7)),
                    reads=[n_aTm] + W3k, writes=[PSK[b2]])
            add("act", lambda q: q.copy(out=Vmn, in_=ps[b2][:NM, 0:128]), reads=[PSK[b2]], writes=[PSK[b2], n_Vmn])

        def c_v(par, g4):
            ntile = NT - par
            b = nb(kinds[1])
            js = [j for j in range(g4 * 4, g4 * 4 + 4) if j < ntile]
            for jj, j in enumerate(js):
                st_ = 64 * par + 128 * j
                for kc in range(8):
                    add("pe", lambda q, kc=kc, jj=jj, st_=st_: q.matmul(ps[b][:, jj * 128:(jj + 1) * 128], lhsT=aT[:, kc, st_:st_ + 128],
                                                                       rhs=W3[:, kc, 256:384], start=(kc == 0), stop=(kc == 7)),
                        reads=[(n_aT, st_ // 128), (n_aT, (st_ + 127) // 128)] + W3k, writes=[PSK[b]])
            nn = len(js)
            add("act", lambda q: q.copy(out=V2[:, par, g4 * 4:g4 * 4 + nn, :], in_=ps[b][:, 0:nn * 128].rearrange("p (j c) -> p j c", j=nn)),
                reads=[PSK[b]], writes=[PSK[b], (n_V2, par, g4)])

        for tb in range(4):
            chunks.append(lambda tb=tb: c_q(tb))
        for tb in range(4):
            chunks.append(lambda tb=tb: c_k(tb))
        chunks.append(c_km)
        for par in range(2):
            for g4 in range(4):
                chunks.append(lambda par=par, g4=g4: c_v(par, g4))
        return chunks

    def zipdirs(two):
        import itertools
        return [it_ for pair in itertools.zip_longest(*two) for it_ in pair if it_ is not None]

    hg_weights_fm(0)
    hg_weights_tm(0)
    hg_front(0)
    hg_weights_fm(1)
    early_, late_ = hg_mid(0)
    emit_list(zipdirs(early_))
    for hd in range(4):
        emit_list(merge(zipdirs(late_), record(hg_front_tm, hd)))
        if hd + 1 < 4:
            hg_weights_tm(hd + 1)
        chain_ops = record(hg_chain, hd)
        if hd + 1 < 4:
            front_ops = record(hg_front, hd + 1)
            emit_list(merge(chain_ops, front_ops))
            if hd + 2 < 4:
                hg_weights_fm(hd + 2)
            early_, late_ = hg_mid(hd + 1)
            emit_list(merge(record(hg_out, hd), zipdirs(early_)))
        else:
            napc = na_proj_chunks(0, kinds=("proj", "nap"))
            napc[0]()
            qk_ops = [o_ for c_ in napc[1:10] for o_ in record(c_)]
            v_ops = [o_ for c_ in napc[10:] for o_ in record(c_)]
            emit_list(merge(chain_ops, qk_ops))
            emit_list(merge(record(hg_out, hd), v_ops))
    if debug:
        tap("yT", yT, [128, 4, T], [(n_yT, hd, g4) for hd in range(4) for g4 in range(4)])
    for n in n_A + n_Bs + n_Cs + [n_qs, n_tq[0]] + n_QT + n_KT + n_Kt + [n_Ktm, n_At, n_St] + n_tgs + [n_Vs[0], n_GGs[0], n_Vms[0]] + n_Z + [n_gf, n_WA, n_WB, n_WC]:
        S.free(n)

    n_naT, naT = S.alloc("naT", [128, 4, T], BF16)
    n_tab, tab = S.alloc("tab", [128, 4, 7, 2, 128], F32)
    for hp in range(4):
        cidx[0] += 1
        add("sp", lambda q, hp=hp: q.dma_start(out=tab[:, hp].rearrange("p a b c -> p (a b c)"), in_=tab_d[:, hp * 1792:(hp + 1) * 1792]),
            writes=[(n_tab, hp)], dma="d_c%d" % cidx[0])
    alloc_nas(1)
    n_sT, sT, n_pT, pT, n_pTm, pTm, n_rd, rden = [], [], [], [], [], [], [], []
    for i in range(2):
        n, a = S.alloc("sT%d" % i, [128, 512], F32); n_sT.append(n); sT.append(a)
        n, a = S.alloc("pT%d" % i, [128, 512], BF16); n_pT.append(n); pT.append(a)
        n, a = S.alloc("pTm%d" % i, [NM, 128], BF16); n_pTm.append(n); pTm.append(a)
        n, a = S.alloc("rden%d" % i, [128, 128], F32); n_rd.append(n); rden.append(a)
    bank_rr.update({"s": [2, 3], "od": [4, 5], "m": [6, 7]})
    for k_ in ("s", "od", "m"):
        bank_ctr[k_] = 0

    def na_rows(hp, side_chunks):
        D_ = NAS[hp % 2]
        qbd, kT, V2, Vmn = D_["qbd"], D_["kT"], D_["V2"], D_["Vmn"]
        n_qbd, n_kT, n_V2, n_Vmn = D_["n_qbd"], D_["n_kT"], D_["n_V2"], D_["n_Vmn"]
        kT_all = [(n_kT, j) for j in range(5)]
        row_banks = {}

        def qk(r):
            rs_ = min(max(r - 4, 0), 24)
            bs = nb("s"); bm = nb("m")
            row_banks[r] = [bs, bm, None]
            rq = qbd[:, r].rearrange("p a c -> p (a c)")
            for i in range(4):
                k0 = NM + rs_ * 64 + i * 128
                add("pe", lambda q, i=i, k0=k0: q.matmul(ps[bs][:, i * 128:(i + 1) * 128], lhsT=kT[:, k0:k0 + 128], rhs=rq, start=True, stop=True),
                    reads=kT_all + [(n_qbd, r // 8)], writes=[PSK[bs]])
            add("pe", lambda q: q.matmul(ps[bm][:NM, 0:128], lhsT=kT[:, 0:NM], rhs=rq, start=True, stop=True),
                reads=kT_all + [(n_qbd, r // 8)], writes=[PSK[bm]])

        def soft(r):
            rs_ = min(max(r - 4, 0), 24)
            bs, bm, _ = row_banks[r]
            s_ = r % 2
            drb0 = rs_ - r + 7
            j0, par = drb0 // 2, drb0 % 2
            add("dve", lambda q: q.tensor_tensor(out=sT[s_].rearrange("p (i c) -> p i c", i=4),
                                                 in0=ps[bs][:, :].rearrange("p (i c) -> p i c", i=4),
                                                 in1=tab[:, hp, j0:j0 + 4, par, :], op=ALU.add),
                reads=[PSK[bs], (n_tab, hp)], writes=[PSK[bs], n_sT[s_]])
            add("act", lambda q: q.activation(out=pT[s_], in_=sT[s_], func=AF.Exp), reads=[n_sT[s_]], writes=[n_pT[s_]])
            add("act", lambda q: q.activation(out=pTm[s_], in_=ps[bm][:NM, 0:128], func=AF.Exp), reads=[PSK[bm]], writes=[PSK[bm], n_pTm[s_]])

        def pv(r):
            rs_ = min(max(r - 4, 0), 24)
            s_ = r % 2
            par, t0 = rs_ % 2, rs_ // 2
            bo = nb("od")
            row_banks[r][2] = bo
            for i in range(4):
                add("pe", lambda q, i=i: q.matmul(ps[bo][:, 0:128], lhsT=V2[:, par, t0 + i, :], rhs=pT[s_][:, i * 128:(i + 1) * 128],
                                                 start=(i == 0), stop=False),
                    reads=[(n_V2, par, (t0 + i) // 4), n_pT[s_]], writes=[PSK[bo]])
            add("pe", lambda q: q.matmul(ps[bo][:, 0:128], lhsT=Vmn[:, :], rhs=pTm[s_][:, :], start=False, stop=True),
                reads=[n_Vmn, n_pTm[s_]], writes=[PSK[bo]])
            for i in range(4):
                add("pe", lambda q, i=i: q.matmul(ps[bo][:, 128:256], lhsT=onesb[:, :], rhs=pT[s_][:, i * 128:(i + 1) * 128],
                                                 start=(i == 0), stop=False),
                    reads=[n_ones, n_pT[s_]], writes=[PSK[bo]])
            add("pe", lambda q: q.matmul(ps[bo][:, 128:256], lhsT=onesb[:NM, :], rhs=pTm[s_][:, :], start=False, stop=True),
                reads=[n_ones, n_pTm[s_]], writes=[PSK[bo]])

        def evac(r):
            s_ = r % 2
            bo = row_banks[r][2]
            add("dve", lambda q: q.reciprocal(out=rden[s_], in_=ps[bo][:, 128:256]), reads=[PSK[bo]], writes=[PSK[bo], n_rd[s_]])
            for h2 in range(2):
                add("dve", lambda q, h2=h2: q.tensor_tensor(out=naT[h2 * 64:(h2 + 1) * 64, hp, r * 64:(r + 1) * 64],
                                                           in0=ps[bo][h2 * 64:(h2 + 1) * 64, h2 * 64:(h2 + 1) * 64],
                                                           in1=rden[s_][h2 * 64:(h2 + 1) * 64, h2 * 64:(h2 + 1) * 64], op=ALU.mult),
                    reads=[PSK[bo], n_rd[s_]], writes=[PSK[bo], (n_naT, hp, r // 8)])

        side = list(side_chunks)
        qk(0)
        qk(1)
        soft(0)
        for r in range(32):
            if r + 2 < 32:
                qk(r + 2)
            if r + 1 < 32:
                soft(r + 1)
            pv(r)
            evac(r)
            if r % 2 == 1 and side:
                side.pop(0)()
        for c_ in side:
            c_()

    for hp in range(4):
        na_rows(hp, na_proj_chunks(hp + 1) if hp + 1 < 4 else [])
    if debug:
        tap("naT", naT, [128, 4, T], [(n_naT, hp, j) for hp in range(4) for j in range(4)])
    for n in [n_tab] + [d_[k_] for d_ in NAS for k_ in ("n_qbd", "n_kT", "n_V2", "n_Vmn", "n_W3")] + n_sT + n_pT + n_pTm + n_rd:
        S.free(n)

    for i in range(2):
        n_abf[i], abf[i] = S.alloc("abf%d" % i, [128, D], BF16)
    n_Wo, Wo = S.alloc("Wo", [128, 8, D], BF16)

    def load_wo(pc):
        load_w2([(Wo[:, :, pc * 256:(pc + 1) * 256], w_o[:, pc * 256:(pc + 1) * 256].rearrange("(kc p) c -> p kc c", p=128))], (n_Wo, pc))
    n_Wup, Wup, n_Wdn, Wdn, n_tr, tr = [], [], [], [], [], []
    n, a = S.alloc("Wup0", [128, 8, 512], BF16); n_Wup.append(n); Wup.append(a)
    n, a = S.alloc("Wdn0", [128, 4, D], BF16); n_Wdn.append(n); Wdn.append(a)

    def load_up(g, pc):
        s_ = g % 2
        load_w2([(Wup[s_][:, :, pc * 256:(pc + 1) * 256],
                  w_up[:, g * 512 + pc * 256:g * 512 + (pc + 1) * 256].rearrange("(kc p) c -> p kc c", p=128))], (n_Wup[s_], pc))

    def load_dn(g, pc):
        s_ = g % 2
        load_w2([(Wdn[s_][:, 2 * pc:2 * pc + 2, :],
                  w_down[g * 512 + pc * 256:g * 512 + (pc + 1) * 256, :].rearrange("(f p) c -> p f c", p=128))], (n_Wdn[s_], pc))

    def load_group(g):
        for pc in range(2):
            load_up(g, pc)
        for pc in range(2):
            load_dn(g, pc)

    extra_pieces = [lambda pc=pc: load_wo(pc) for pc in range(4)] + [lambda: load_up(0, 0), lambda: load_up(0, 1), lambda: load_dn(0, 0), lambda: load_dn(0, 1)]
    n_mixT, mixT = S.alloc("mixT", [128, 8, T], BF16)
    n_Wg, Wg, n_Wno, Wno, n_Who, Who = [], [], [], [], [], []
    n_sg, sg, n_tt, tt = [], [], [], []
    for i in range(2):
        n, a = S.alloc("Wg%d" % i, [128, 8, 256], BF16); n_Wg.append(n); Wg.append(a)
        n, a = S.alloc("Wno%d" % i, [128, 2, 4, 128], BF16); n_Wno.append(n); Wno.append(a)
    for i in range(4):
        n, a = S.alloc("sg%d" % i, [128, 512], F32); n_sg.append(n); sg.append(a)
        n, a = S.alloc("tt%d" % i, [128, 512], F32); n_tt.append(n); tt.append(a)
    def load_3a(dc):
        s_ = dc % 2
        load_w2([(Wg[s_][:, :, jj * 128:(jj + 1) * 128], win_cols(c0 + dc * 128, 128)) for jj, c0 in enumerate((C_GNA, C_GHG))], n_Wg[s_])
        load_w2([(Wno[s_][:, jj], wsrc[:, dc * 128:(dc + 1) * 128].rearrange("(kc p) c -> p kc c", p=128))
                 for jj, wsrc in enumerate((w_na_out, w_hg_out))], n_Wno[s_])

    it = 0
    for dc in range(8):
        s_ = dc % 2
        if dc == 0:
            load_3a(0)
        if dc + 1 < 8:
            load_3a(dc + 1)
        extra_pieces[dc]()
        for tb in range(4):
            bb = 4 * (it % 2)
            it += 1
            for j in range(2):
                for kc in range(8):
                    add("pe", lambda q, kc=kc, j=j, bb=bb, tb=tb, s_=s_: q.matmul(ps[bb + j][:, :], lhsT=Wg[s_][:, kc, j * 128:(j + 1) * 128],
                                                                                 rhs=aT[:, kc, tb * 512:(tb + 1) * 512], start=(kc == 0), stop=(kc == 7)),
                        reads=aT_keys(tb) + [n_Wg[s_]], writes=[PSK[bb + j]])
            for j, (src, skey) in enumerate(((naT, n_naT), (yT, n_yT))):
                for kc in range(4):
                    add("pe", lambda q, kc=kc, j=j, bb=bb, tb=tb, s_=s_, src=src: q.matmul(ps[bb + 2 + j][:, :], lhsT=Wno[s_][:, j, kc, :],
                                                                                          rhs=src[:, kc, tb * 512:(tb + 1) * 512], start=(kc == 0), stop=(kc == 3)),
                        reads=[(skey, kc, tb), n_Wno[s_]], writes=[PSK[bb + 2 + j]])
            for j in range(2):
                k_ = 2 * (it % 2) + j
                add("act", lambda q, bb=bb, j=j, k_=k_: q.activation(out=sg[k_], in_=ps[bb + j][:, :], func=AF.Sigmoid),
                    reads=[PSK[bb + j]], writes=[PSK[bb + j], n_sg[k_]])
                add("dve", lambda q, bb=bb, j=j, k_=k_: q.tensor_tensor(out=tt[k_], in0=ps[bb + 2 + j][:, :], in1=sg[k_], op=ALU.mult),
                    reads=[PSK[bb + 2 + j], n_sg[k_]], writes=[PSK[bb + 2 + j], n_tt[k_]])
            k0 = 2 * (it % 2)
            add("pool", lambda q, k0=k0, dc=dc, tb=tb: q.tensor_tensor(out=mixT[:, dc, tb * 512:(tb + 1) * 512], in0=tt[k0], in1=tt[k0 + 1], op=ALU.add),
                reads=[n_tt[k0], n_tt[k0 + 1]], writes=[(n_mixT, dc, tb)])
    if debug:
        tap("mixT", mixT, [128, 8, T], [(n_mixT, dc, tb) for dc in range(8) for tb in range(4)])
    for n in [n_aT, n_aTm, n_naT, n_yT] + n_Wg + n_Wno + n_sg + n_tt:
        S.free(n)

    cidx[0] += 1
    add("sp", lambda q: q.dma_start(out=gb, in_=norm_mlp.partition_broadcast(128)), writes=[n_gb], dma="d_c%d" % cidx[0])
    n_h, h = S.alloc("h", [128, NT, D], F32)
    for i in range(NT):
        add("sp", lambda q, i=i: q.dma_start(out=h[:, i, :], in_=x[i * 128:(i + 1) * 128, :]), writes=[(n_h, i, 0), (n_h, i, 1)], dma="d_h%d" % i)
    n_mT, mT = S.alloc("mT", [128, 8, T], BF16)
    bctr = 0
    pending = []
    for i in range(NT):
        for ch in range(2):
            b = 2 + bctr % 6
            bctr += 1
            for dc in range(8):
                add("pe", lambda q, dc=dc, b=b, i=i, ch=ch: q.matmul(ps[b][:, :], lhsT=mixT[:, dc, i * 128:(i + 1) * 128], rhs=Wo[:, dc, ch * 512:(ch + 1) * 512],
                                                                    start=(dc == 0), stop=(dc == 7)),
                    reads=[(n_mixT, dc, i // 4), (n_Wo, 2 * ch), (n_Wo, 2 * ch + 1)], writes=[PSK[b]])
            add("dve", lambda q, b=b, i=i, ch=ch: q.tensor_tensor(out=h[:, i, ch * 512:(ch + 1) * 512], in0=h[:, i, ch * 512:(ch + 1) * 512], in1=ps[b][:, :], op=ALU.add),
                reads=[PSK[b], (n_h, i, ch)], writes=[PSK[b], (n_h, i, ch)])
        add("act", lambda q, i=i: q.activation(out=junk, in_=h[:, i, :], func=AF.Square, accum_out=rs[:, 40 + i:41 + i]),
            reads=[(n_h, i, 0), (n_h, i, 1)], writes=[n_junk, (n_rs, 40 + i)])
        if pending:
            j = pending.pop(0)
            norm_apply(j, h[:, j, :], [(n_h, j, 0), (n_h, j, 1)], 128, 40 + j, lambda j=j: mT[:, :, j * 128:(j + 1) * 128], (n_mT, j), j % 2, n_gb)
        if i % 4 == 3:
            rstd_ops(40 + i - 3, 4, 128, 1.0 / D)
            pending += [i - 3, i - 2, i - 1, i]
    for j in pending:
        norm_apply(j, h[:, j, :], [(n_h, j, 0), (n_h, j, 1)], 128, 40 + j, lambda j=j: mT[:, :, j * 128:(j + 1) * 128], (n_mT, j), j % 2, n_gb)
    if debug:
        tap("h1", h, [128, NT, D], [(n_h, i, ch) for i in range(NT) for ch in range(2)])
    S.free(n_mixT)
    S.free(n_Wo)

    n_uT, uT = S.alloc("uT", [128, 4, T], BF16)
    n, a = S.alloc("Wup1", [128, 8, 512], BF16); n_Wup.append(n); Wup.append(a)
    n, a = S.alloc("Wdn1", [128, 4, D], BF16); n_Wdn.append(n); Wdn.append(a)
    for i in range(3):
        n, a = S.alloc("tr%d" % i, [128, 512], F32); n_tr.append(n); tr.append(a)
    cidx[0] += 1
    add("sp", lambda q: q.dma_start(out=gb, in_=norm_final.partition_broadcast(128)), writes=[n_gb], dma="d_c%d" % cidx[0])
    bank_rr.update({"up": [2, 3, 4], "dn": [5, 6, 7]})
    bank_ctr["up"] = 0
    bank_ctr["dn"] = 0
    trc = 0
    NG = DFF // 512
    for g in range(NG):
        s_ = g % 2
        for fcl in range(4):
            for tb in range(4):
                b = nb("up")
                for kc in range(8):
                    add("pe", lambda q, kc=kc, b=b, fcl=fcl, tb=tb, s_=s_: q.matmul(ps[b][:, :], lhsT=Wup[s_][:, kc, fcl * 128:(fcl + 1) * 128],
                                                                                   rhs=mT[:, kc, tb * 512:(tb + 1) * 512], start=(kc == 0), stop=(kc == 7)),
                        reads=[(n_mT, 4 * tb + j) for j in range(4)] + [(n_Wup[s_], fcl // 2)], writes=[PSK[b]])
                k_ = trc % 3
                trc += 1
                add("act", lambda q, b=b, k_=k_: q.activation(out=tr[k_], in_=ps[b][:, :], func=AF.Relu), reads=[PSK[b]], writes=[PSK[b], n_tr[k_]])
                add("pool", lambda q, k_=k_, fcl=fcl, tb=tb: q.tensor_tensor(out=uT[:, fcl, tb * 512:(tb + 1) * 512], in0=tr[k_], in1=tr[k_], op=ALU.mult),
                    reads=[n_tr[k_]], writes=[(n_uT, fcl, tb)])
        if g + 1 < NG:
            load_group(g + 1)
        for i in range(NT):
            for ch in range(2):
                b = nb("dn")
                for fcl in range(4):
                    add("pe", lambda q, fcl=fcl, b=b, i=i, ch=ch, s_=s_: q.matmul(ps[b][:, :], lhsT=uT[:, fcl, i * 128:(i + 1) * 128],
                                                                                 rhs=Wdn[s_][:, fcl, ch * 512:(ch + 1) * 512], start=(fcl == 0), stop=(fcl == 3)),
                        reads=[(n_uT, fcl, i // 4), (n_Wdn[s_], fcl // 2)], writes=[PSK[b]])
                add("dve", lambda q, b=b, i=i, ch=ch: q.tensor_tensor(out=h[:, i, ch * 512:(ch + 1) * 512], in0=h[:, i, ch * 512:(ch + 1) * 512], in1=ps[b][:, :], op=ALU.add),
                    reads=[PSK[b], (n_h, i, ch)], writes=[PSK[b], (n_h, i, ch)])
            if g == NG - 1:
                rc = 60 + i
                hk = [(n_h, i, 0), (n_h, i, 1)]
                add("act", lambda q, i=i, rc=rc: q.activation(out=junk, in_=h[:, i, :], func=AF.Square, accum_out=rs[:, rc:rc + 1]),
                    reads=hk, writes=[n_junk, (n_rs, rc)])
                rstd_ops(rc, 1, 128, 1.0 / D)
                add("act", lambda q, i=i, rc=rc: q.activation(out=h[:, i, :], in_=h[:, i, :], func=AF.Copy, scale=rs[:, rc:rc + 1]),
                    reads=hk + [(n_rs, rc)], writes=hk)
                add("pool", lambda q, i=i: q.tensor_tensor(out=h[:, i, :], in0=h[:, i, :], in1=gb, op=ALU.mult),
                    reads=hk + [n_gb], writes=hk)
                add("sp", lambda q, i=i: q.dma_start(out=out[i * 128:(i + 1) * 128, :], in_=h[:, i, :]), reads=hk, dma="d_out")
    return nc, S, dbg, ["d_out"] + dbg_sems


def finish_program(nc, S, final_waits):
    from contextlib import ExitStack
    S.finalize()
    with ExitStack() as es:
        sems = {}
        for k in ["pe", "act", "dve", "pool"] + S.dma_sems:
            sems[k] = es.enter_context(nc.semaphore(k))
        block = es.enter_context(nc.Block())

        @block.sync
        def _(q):
            S.emit("sp", q, sems)
            for k in final_waits:
                q.wait_ge(sems[k], S.final_counts[k])

        @block.tensor
        def _(q):
            S.emit("pe", q, sems)

        @block.scalar
        def _(q):
            S.emit("act", q, sems)

        @block.vector
        def _(q):
            S.emit("dve", q, sems)

        @block.gpsimd
        def _(q):
            S.emit("pool", q, sems)
    return nc


def host_inputs(x, meta_tokens, w_in, w_na_out, w_hg_out, w_o, w_up, w_down,
                norm_mix, norm_mlp, norm_final, hg_norm, na_rpb, hg_lb_logits):
    f = lambda a: np.ascontiguousarray(np.asarray(a, dtype=np.float32))
    lbl = f(np.asarray(hg_lb_logits).reshape(2, 2, 4, 128).transpose(3, 0, 1, 2).reshape(128, 16))
    rpb = np.asarray(na_rpb, dtype=np.float32)[0]
    p = np.arange(128)
    jj, w = p // 64, p % 64
    c = np.arange(64)
    cs = np.clip(c - 8, 0, 48)
    in_win = (w[:, None] >= cs[None, :]) & (w[:, None] < cs[None, :] + 16)
    dc = np.clip(w[:, None] - c[None, :], -15, 15) + 15
    tab = np.empty((128, 4, 7, 2, 2, 64), np.float32)
    for hp in range(4):
        for j in range(7):
            for par in range(2):
                drb = 2 * j + par
                dr_idx = drb + jj
                for h2 in range(2):
                    vals = rpb[hp * 2 + h2][dr_idx[:, None], dc]
                    tab[:, hp, j, par, h2, :] = np.where(in_win, vals, np.float32(-1e30))
    tri = np.tril(np.ones((128, 128), np.uint8))
    mf = np.ascontiguousarray(tri.T)
    mb = np.ascontiguousarray(tri)
    masks = np.concatenate([np.tile(mf, (1, 4)), np.tile(mb, (1, 4))], axis=1).astype(np.uint8)
    common = {
        "meta": f(meta_tokens), "w_in": f(np.asarray(w_in)[0]), "w_na_out": f(np.asarray(w_na_out)[0]),
        "w_hg_out": f(np.asarray(w_hg_out)[0]), "w_o": f(np.asarray(w_o)[0]), "w_up": f(np.asarray(w_up)[0]),
        "w_down": f(np.asarray(w_down)[0]), "norm_mix": f(np.asarray(norm_mix)[0]), "norm_mlp": f(np.asarray(norm_mlp)[0]),
        "norm_final": f(norm_final), "hg_norm": f(np.asarray(hg_norm)[0]), "lbl": lbl,
        "tab": np.ascontiguousarray(tab.reshape(128, -1)), "ident": np.eye(128, dtype=np.float32), "masks": masks,
    }
    xs = np.asarray(x, dtype=np.float32)
    return [dict(common, x=np.ascontiguousarray(xs[b])) for b in range(xs.shape[0])]


_CACHE = {}


def kernel(x, meta_tokens, w_in, w_na_out, w_hg_out, w_o, w_up, w_down,
           norm_mix, norm_mlp, norm_final, hg_norm, na_rpb, hg_lb_logits):
    maps = host_inputs(x, meta_tokens, w_in, w_na_out, w_hg_out, w_o, w_up, w_down,
                       norm_mix, norm_mlp, norm_final, hg_norm, na_rpb, hg_lb_logits)
    nc, S, _dbg, fw = build_program(debug=False)
    finish_program(nc, S, fw)
    n = len(maps)
    res = run_bass_kernel_spmd(nc, maps, core_ids=list(range(n)))
    return np.stack([np.asarray(r["out"], dtype=np.float32) for r in res.results], axis=0)
```
